# Optimizing a Trainium2 kernel written in Bass

```python
import math
import jax, jax.numpy as jnp
from jax import lax
import numpy as np

D_MODEL = 1024
BATCH = 16
SEQ = 2048
DEPTH = 1

HEAD_DIM = 64
ATTN_WIDTH = D_MODEL // 2
N_ATTN_HEADS = ATTN_WIDTH // HEAD_DIM
DILATED_CONFIGS = ((128, 1), (512, 4), (2048, 16))
SGU_WIDTH = D_MODEL // 4
N_SGU_GROUPS = 4
SGU_GROUP = SGU_WIDTH // N_SGU_GROUPS
SGU_CHUNK = 128
MEM_WIDTH = D_MODEL // 4
N_MEM_HEADS = 4
MEM_HEAD_DIM = MEM_WIDTH // N_MEM_HEADS
N_MEM = 256
MIX_WIDTH = ATTN_WIDTH + SGU_WIDTH + MEM_WIDTH
IN_COLS = 4 * ATTN_WIDTH + 3 * SGU_WIDTH + 2 * MEM_WIDTH
EPS = 1e-6
NEG_INF = -1e30

kernel_name = "hybrid_dilated_sgu_memory_encoder"


def rmsnorm(x, g):
    xf = x.astype(jnp.float32)
    y = xf * lax.rsqrt(jnp.mean(xf * xf, axis=-1, keepdims=True) + EPS) * g.astype(jnp.float32)
    return y.astype(x.dtype)


def alibi_slopes(n):
    return jnp.power(2.0, -8.0 * (jnp.arange(n, dtype=jnp.float32) + 1.0) / n)


def dilated_window_attention(q, k, v, slopes, window, dilation):
    B, S, H, E = q.shape
    radius = window // (2 * dilation)
    blk = radius
    L = S // dilation
    N = B * dilation
    nb = -(-L // blk)
    Lp = nb * blk

    def to_sub(t):
        return t.reshape(B, L, dilation, H, E).transpose(0, 2, 1, 3, 4).reshape(N, L, H, E)

    qs = jnp.pad(to_sub(q), ((0, 0), (0, Lp - L), (0, 0), (0, 0))).reshape(N, nb, blk, H, E)

    def neighbour_blocks(t):
        tp = jnp.pad(to_sub(t), ((0, 0), (blk, Lp - L + blk), (0, 0), (0, 0))).reshape(N, nb + 2, blk, H, E)
        return jnp.concatenate([tp[:, :-2], tp[:, 1:-1], tp[:, 2:]], axis=2)

    kb = neighbour_blocks(k)
    vb = neighbour_blocks(v)
    q_idx = jnp.arange(Lp).reshape(nb, blk)
    k_idx = (jnp.arange(nb) * blk)[:, None] - blk + jnp.arange(3 * blk)[None, :]
    rel = jnp.abs(k_idx[:, None, :] - q_idx[:, :, None])
    valid = (rel <= radius) & (k_idx[:, None, :] >= 0) & (k_idx[:, None, :] < L)
    dist = (rel * dilation).astype(jnp.float32)

    s = jnp.einsum('nbqhe,nbkhe->nhbqk', qs.astype(jnp.float32), kb.astype(jnp.float32)) * (E ** -0.5)
    s = s - slopes[None, :, None, None, None] * dist[None, None]
    s = jnp.where(valid[None, None], s, NEG_INF)
    m = jnp.max(s, axis=-1, keepdims=True)
    p = jnp.exp(s - m)
    l = jnp.sum(p, axis=-1)
    o = jnp.einsum('nhbqk,nbkhe->nbqhe', p, vb.astype(jnp.float32))
    o = o / jnp.transpose(l, (0, 2, 3, 1))[..., None]
    lse = m[..., 0] + jnp.log(l)

    o = o.reshape(N, Lp, H, E)[:, :L].reshape(B, dilation, L, H, E).transpose(0, 2, 1, 3, 4).reshape(B, S, H, E)
    lse = lse.reshape(N, H, Lp)[..., :L].reshape(B, dilation, H, L).transpose(0, 3, 1, 2).reshape(B, S, H)
    return o, lse


def mixture_of_dilations(q, k, v):
    slopes = alibi_slopes(q.shape[2])
    outs, lses = [], []
    for window, dilation in DILATED_CONFIGS:
        o, lse = dilated_window_attention(q, k, v, slopes, window, dilation)
        outs.append(o)
        lses.append(lse)
    w = jax.nn.softmax(jnp.stack(lses, axis=0), axis=0)
    return jnp.sum(w[..., None] * jnp.stack(outs, axis=0), axis=0)


def chunked_spatial_gating(u, v, g_v, w_s, b_s):
    B, S, _ = v.shape
    n = S // SGU_CHUNK
    vn = rmsnorm(v, g_v).reshape(B, n, SGU_CHUNK, N_SGU_GROUPS, SGU_GROUP).astype(jnp.float32)
    mixed = jnp.einsum('gts,bnsgc->bntgc', w_s.astype(jnp.float32), vn)
    mixed = mixed + b_s.astype(jnp.float32).T[None, None, :, :, None]
    return u.astype(jnp.float32) * mixed.reshape(B, S, SGU_WIDTH)


def memory_cross_attention(q, mem, g_mem, w_mem_kv):
    B, M, _ = mem.shape
    kv = rmsnorm(mem, g_mem) @ w_mem_kv
    k, v = jnp.split(kv, 2, axis=-1)
    k = k.reshape(B, M, N_MEM_HEADS, MEM_HEAD_DIM).astype(jnp.float32)
    v = v.reshape(B, M, N_MEM_HEADS, MEM_HEAD_DIM).astype(jnp.float32)
    s = jnp.einsum('bshe,bmhe->bhsm', q.astype(jnp.float32), k) * (MEM_HEAD_DIM ** -0.5)
    p = jax.nn.softmax(s, axis=-1)
    return jnp.einsum('bhsm,bmhe->bshe', p, v)


def hybrid_layer(x, mem, g_norm, w_in, w_s, b_s, g_v, g_mem, w_mem_kv, w_out):
    B, S, _ = x.shape
    h = rmsnorm(x, g_norm)
    proj = h @ w_in
    offs = np.cumsum([ATTN_WIDTH] * 4 + [SGU_WIDTH] * 3 + [MEM_WIDTH])
    qa, ka, va, za, ub, vb, zb, qm, zm = jnp.split(proj, list(offs), axis=-1)

    shp = (B, S, N_ATTN_HEADS, HEAD_DIM)
    a = mixture_of_dilations(qa.reshape(shp), ka.reshape(shp), va.reshape(shp)).reshape(B, S, ATTN_WIDTH)
    sg = chunked_spatial_gating(jax.nn.gelu(ub), jax.nn.gelu(vb), g_v, w_s, b_s)
    mo = memory_cross_attention(qm.reshape(B, S, N_MEM_HEADS, MEM_HEAD_DIM), mem, g_mem, w_mem_kv)
    mo = mo.reshape(B, S, MEM_WIDTH)

    gated = jnp.concatenate([
        jax.nn.silu(za.astype(jnp.float32)) * a,
        jax.nn.silu(zb.astype(jnp.float32)) * sg,
        jax.nn.silu(zm.astype(jnp.float32)) * mo,
    ], axis=-1).astype(x.dtype)
    return x + gated @ w_out


def setup_inputs(seed: int = 0) -> dict:
    key = jax.random.key(seed)
    ks = jax.random.split(key, 12)
    f32 = jnp.float32
    x = jax.random.normal(ks[0], (BATCH, SEQ, D_MODEL), f32)
    mem = jax.random.normal(ks[1], (BATCH, N_MEM, D_MODEL), f32)
    g_norm = 1.0 + 0.02 * jax.random.normal(ks[2], (DEPTH, D_MODEL), f32)
    w_in = jax.random.normal(ks[3], (DEPTH, D_MODEL, IN_COLS), f32) * D_MODEL ** -0.5
    w_sgu_spatial = jax.random.normal(ks[4], (DEPTH, N_SGU_GROUPS, SGU_CHUNK, SGU_CHUNK), f32) * (0.5 * SGU_CHUNK ** -0.5)
    b_sgu_spatial = 1.0 + 0.02 * jax.random.normal(ks[5], (DEPTH, N_SGU_GROUPS, SGU_CHUNK), f32)
    g_sgu_v = 1.0 + 0.02 * jax.random.normal(ks[6], (DEPTH, SGU_WIDTH), f32)
    g_mem = 1.0 + 0.02 * jax.random.normal(ks[7], (DEPTH, D_MODEL), f32)
    w_mem_kv = jax.random.normal(ks[8], (DEPTH, D_MODEL, 2 * MEM_WIDTH), f32) * D_MODEL ** -0.5
    w_out = jax.random.normal(ks[9], (DEPTH, MIX_WIDTH, D_MODEL), f32) * MIX_WIDTH ** -0.5
    g_final = 1.0 + 0.02 * jax.random.normal(ks[10], (D_MODEL,), f32)
    return {"x": x, "mem": mem, "g_norm": g_norm, "w_in": w_in,
            "w_sgu_spatial": w_sgu_spatial, "b_sgu_spatial": b_sgu_spatial, "g_sgu_v": g_sgu_v,
            "g_mem": g_mem, "w_mem_kv": w_mem_kv, "w_out": w_out, "g_final": g_final}


def reference(x, mem, g_norm, w_in, w_sgu_spatial, b_sgu_spatial, g_sgu_v, g_mem, w_mem_kv, w_out, g_final):
    h = x
    for layer in range(DEPTH):
        h = hybrid_layer(h, mem, g_norm[layer], w_in[layer], w_sgu_spatial[layer], b_sgu_spatial[layer],
                         g_sgu_v[layer], g_mem[layer], w_mem_kv[layer], w_out[layer])
    return rmsnorm(h, g_final)
```

```python
import numpy as np
_KN = {"RA": 0.8, "RM": 3.0, "STEPK": 4, "MB4": 0}
_K = lambda n, d: float(_KN.get(n, d))
from contextlib import ExitStack
import concourse.bass as bass
import concourse.mybir as mybir
from concourse.bass_utils import run_bass_kernel_spmd

F32 = mybir.dt.float32
BF16 = mybir.dt.bfloat16
AF = mybir.ActivationFunctionType
ALU = mybir.AluOpType

NCORES = 8
SEQ = 2048
D = 1024
NSEQ = 2
INC = 3328
NMEM = 256
EPS = 1e-6
GELU_C = 0.7978845608028654
DILS = (1, 4, 16)


class Tracker:
    def __init__(self, nc, es):
        self.nc = nc
        self.es = es
        self.eng = {"pe": nc.tensor, "act": nc.scalar, "dve": nc.vector, "pool": nc.gpsimd, "sp": nc.sync}
        self.sems = {}
        self.waited = {}
        self.lastw = {}
        self.readers = {}
        self.nins = 0

    def _sem(self, name):
        if name not in self.sems:
            self.sems[name] = [self.es.enter_context(self.nc.semaphore(name.replace(":", "_"))), 0]
        return self.sems[name]

    def _wait(self, engname, ticket):
        semname, val = ticket
        if semname == engname and engname == "pe":
            return
        cur = self.sems[semname][1]
        assert val <= cur, ("wait on unsignalled ticket", engname, ticket, cur)
        k = (engname, semname)
        if self.waited.get(k, 0) >= val:
            return
        self.eng[engname].wait_ge(self.sems[semname][0], val)
        self.waited[k] = val
        self.nins += 1

    def _deps(self, engname, reads, writes, me):
        for key in reads:
            t = self.lastw.get(key)
            if t is not None:
                self._wait(engname, t)
        for key in writes:
            t = self.lastw.get(key)
            if t is not None and t[0] != me:
                self._wait(engname, t)
            for sn, v in self.readers.get(key, {}).items():
                if sn != me:
                    self._wait(engname, (sn, v))

    def _record(self, ticket, reads, writes):
        for key in reads:
            d = self.readers.setdefault(key, {})
            if d.get(ticket[0], 0) < ticket[1]:
                d[ticket[0]] = ticket[1]
        for key in writes:
            self.lastw[key] = ticket
            self.readers[key] = {}

    def op(self, engname, fn, reads=(), writes=(), sig=True):
        self._deps(engname, reads, writes, engname)
        ins = fn(self.eng[engname])
        s = self._sem(engname)
        if sig:
            s[1] += 1
            ins.then_inc(s[0], 1)
            ticket = (engname, s[1])
        else:
            ticket = (engname, s[1] + 1)
        self._record(ticket, reads, writes)
        self.nins += 1
        return ins

    def dma(self, qname, out, in_, reads=(), writes=(), semkey="d"):
        self._deps(qname, reads, writes, None)
        ins = self.eng[qname].dma_start(out=out, in_=in_)
        s = self._sem("dma:" + semkey)
        s[1] += 16
        ins.then_inc(s[0], 16)
        self._record(("dma:" + semkey, s[1]), reads, writes)
        self.nins += 1
        return ins

    def finish(self, qname, semkeys):
        for sk in semkeys:
            s = self.sems.get("dma:" + sk)
            if s is not None:
                self.eng[qname].wait_ge(s[0], s[1])


def build_program():
    nc = bass.Bass("TRN2", target_bir_lowering=False)
    dt_in = lambda name, shape: nc.dram_tensor(name, shape, F32, kind="ExternalInput").ap()
    x_d = dt_in("x", [NSEQ * SEQ, D])
    mem_d = dt_in("mem", [NSEQ * NMEM, D])
    g_norm_d = dt_in("g_norm", [D])
    w_in_d = dt_in("w_in", [D, INC])
    w_s_d = dt_in("w_s", [4, 128, 128])
    b_s_d = dt_in("b_s", [4, 128])
    g_v_d = dt_in("g_v", [256])
    g_mem_d = dt_in("g_mem", [D])
    w_kv_d = dt_in("w_kv", [D, 512])
    w_out_d = dt_in("w_out", [D, D])
    g_fin_d = dt_in("g_final", [D])
    ident_d = dt_in("ident", [128, 128])
    relabs_d = dt_in("relabs", [128, 256])
    band_d = dt_in("band", [128, 256])
    out_d = nc.dram_tensor("out", [NSEQ * SEQ, D], F32, kind="ExternalOutput").ap()

    with ExitStack() as es:
        T = Tracker(nc, es)

        def sb(name, shape, dt):
            return es.enter_context(nc.sbuf_tensor(name, shape, dt))

        identb = sb("identb", [128, 128], BF16)
        gn_bc = sb("gn_bc", [128, D], F32)
        gfin_bc = sb("gfin_bc", [128, D], F32)
        wout = sb("wout", [128, 8, D], BF16)
        wkv = sb("wkv", [128, 8, 512], BF16)
        WsT = sb("WsT", [128, 4, 128], BF16)
        bsT = sb("bsT", [128, 2, 128], F32)
        gvT = sb("gvT", [128, 2], F32)
        gmT = sb("gmT", [128, 8], F32)
        masks = sb("masks", [128, 12, 2, 256], BF16)
        xin = [sb(f"xin{i}", [128, 2048], F32) for i in range(2)]
        hT = sb("hT", [128, 8, SEQ], BF16)
        gatedT = sb("gatedT", [128, 8, SEQ], BF16)
        work = [sb(f"work{i}", [128, SEQ], BF16) for i in range(8)]
        wbf = [sb(f"wbf{i}", [128, 8, 128], BF16) for i in range(3)]
        vtok = sb("vtok", [128, 48, 192], BF16)
        ebuf = [sb(f"ebuf{i}", [128, 512], BF16) for i in range(4)]
        tmp = [sb(f"tmp{i}", [128, 512], F32) for i in range(4)]
        kmT = sb("kmT", [128, 2, NMEM], BF16)
        vmtok = sb("vmtok", [128, 2, 2, 192], BF16)
        rL = sb("rL", [128, 2048], F32)
        ss = sb("ss", [128, 16], F32)
        lnv = sb("lnv", [128, 16], F32)
        rstd = sb("rstd", [128, 16], F32)
        ss2 = sb("ss2", [128, 16], F32)
        lnv2 = sb("lnv2", [128, 16], F32)
        rstd2 = sb("rstd2", [128, 16], F32)
        ssm = sb("ssm", [128, 2], F32)
        lnvm = sb("lnvm", [128, 2], F32)
        rstdm = sb("rstdm", [128, 2], F32)
        sso = sb("sso", [128, 16], F32)
        lnvo = sb("lnvo", [128, 16], F32)
        rstdo = sb("rstdo", [128, 16], F32)
        PS = es.enter_context(nc.psum_tensor("PS", [128, 4096], F32))
        PSb = PS.bitcast(BF16)

        def bank(i):
            return PS[:, 512 * i:512 * (i + 1)]

        def bankb(i):
            return PSb[:, 1024 * i:1024 * (i + 1)]

        junk = tmp[0].bitcast(BF16)
        vraw_v = xin[0].bitcast(BF16)[:, :].rearrange("p (t c) -> p t c", c=256)

        def setup_early():
            T.dma("sp", out=tmp[0][:, 0:128], in_=ident_d, writes=[("tmp", 0)], semkey="c0")
            T.op("dve", lambda e: e.tensor_copy(out=identb[:], in_=tmp[0][:, 0:128]), reads=[("tmp", 0)], writes=["identb"])
            T.dma("sp", out=gn_bc[:], in_=g_norm_d.partition_broadcast(128), writes=["gn_bc"], semkey="c1")

        def setup_mid():
            T.dma("sp", out=gfin_bc[:], in_=g_fin_d.partition_broadcast(128), writes=["gfin_bc"], semkey="c2")
            T.dma("sp", out=gmT[:], in_=g_mem_d.rearrange("(p k) -> p k", k=8), writes=["gmT"], semkey="c3")
            for j in range(2):
                T.dma("sp", out=gvT[:, j:j + 1], in_=g_v_d[j * 128:(j + 1) * 128].rearrange("(p o) -> p o", o=1),
                      writes=["gvT"], semkey="c4")
            for g in range(4):
                j, gg = g // 2, g % 2
                T.dma("sp", out=bsT[64 * gg:64 * gg + 64, j, :], in_=b_s_d[g, :].partition_broadcast(64),
                      writes=["bsT"], semkey="c5")
            T.op("pool", lambda e: e.memset(vtok[:], 1.0), writes=["vtok"])
            T.op("pool", lambda e: e.memset(vmtok[:], 1.0), writes=["vmtok"])
            T.dma("sp", out=tmp[0][:, 0:512].rearrange("p (g s) -> p g s", g=4), in_=w_s_d.rearrange("g t s -> t g s"),
                  writes=[("tmp", 0)], semkey="c0")
            T.op("dve", lambda e: e.tensor_copy(out=ebuf[0][:], in_=tmp[0][:]), reads=[("tmp", 0)], writes=[("ebuf", 0)])
            for g in range(4):
                T.op("pe", lambda e, g=g: e.transpose(out=bankb(7)[:, 128 * g:128 * (g + 1)], in_=ebuf[0][:, 128 * g:128 * (g + 1)], identity=identb[:]),
                     reads=[("ebuf", 0), "identb"], writes=[("ps", 7)], sig=(g == 3))
            T.op("act", lambda e: e.activation(out=WsT[:].rearrange("p g t -> p (g t)"), in_=bankb(7)[:, 0:512], func=AF.Copy),
                 reads=[("ps", 7)], writes=["WsT"])
            for i in range(2):
                st = xin[i][:].rearrange("p (k n) -> p k n", k=4)
                T.dma("sp", out=st, in_=w_kv_d.rearrange("(p k) n -> p k n", k=8)[:, 4 * i:4 * i + 4, :],
                      writes=[("xin", i)], semkey=f"xin{i}")
                for kk in range(4):
                    k = 4 * i + kk
                    T.op("dve", lambda e, st=st, kk=kk, k=k: e.tensor_scalar(out=wkv[:, k, :], in0=st[:, kk, :], scalar1=gmT[:, k:k + 1], scalar2=None, op0=ALU.mult),
                         reads=[("xin", i), "gmT"], writes=["wkv"])

        def setup_late():
            T.dma("sp", out=rL[:, 0:256], in_=relabs_d, writes=[("rL", 0)], semkey="c6")
            T.dma("sp", out=rL[:, 256:512], in_=band_d, writes=[("rL", 0)], semkey="c6")
            for c in range(3):
                for hp in range(4):
                    for hh in range(2):
                        h = 2 * hp + hh
                        coef = -(2.0 ** (-(h + 1))) * DILS[c]
                        ms = 1 + hh
                        tsl = rL[:, 512 * ms:512 * ms + 256]
                        T.op("act", lambda e, tsl=tsl, coef=coef: e.activation(out=tsl, in_=rL[:, 0:256], func=AF.Exp, scale=coef),
                             reads=[("rL", 0)], writes=[("rL", ms)])
                        T.op("dve", lambda e, tsl=tsl, c=c, hp=hp, hh=hh: e.tensor_tensor(out=masks[:, c * 4 + hp, hh, :], in0=tsl, in1=rL[:, 256:512], op=ALU.mult),
                             reads=[("rL", ms), ("rL", 0)], writes=["masks"])
                    yield
            for i in range(4):
                st = xin[i % 2][:].rearrange("p (k n) -> p k n", k=2)
                T.dma("sp", out=st, in_=w_out_d[256 * i:256 * (i + 1), :].rearrange("(k p) n -> p k n", p=128),
                      writes=[("xin", i % 2)], semkey=f"xin{i % 2}")
                fac = 0.25 if i == 2 else 0.5
                T.op("act", lambda e, st=st, i=i, fac=fac: e.activation(out=wout[:, 2 * i:2 * i + 2, :], in_=st, func=AF.Copy, scale=fac),
                     reads=[("xin", i % 2)], writes=["wout"])
                yield

        chunk_order = []
        for hp in range(4):
            chunk_order += [hp, 4 + hp, 8 + hp, 12 + hp]
        chunk_order += [22, 23, 24, 25]
        chunk_order += [18, 19, 16, 17, 20, 21]
        stream = chunk_order * NSEQ
        wstate = {"loaded": 0}

        def load_chunk(i):
            if i >= len(stream):
                return
            c = stream[i]
            sl = i % 3
            T.dma("pool", out=wbf[sl][:], in_=w_in_d[:, 128 * c:128 * (c + 1)].rearrange("(p k) c -> p k c", k=8),
                  writes=[("wbf", sl)], semkey=f"wbf{sl}")

        def next_weights():
            i = wstate["loaded"]
            if i == 0:
                load_chunk(0)
                load_chunk(1)
            load_chunk(i + 2)
            wstate["loaded"] = i + 1
            return i % 3

        cnt = {"tmp": 0, "evac": 0, "pb": 0}

        def gen_proj(kind, dst, dst_key):
            sl = next_weights()
            for n in range(4):
                pbl = cnt.get("pbanks", (0, 1))
                b = pbl[cnt["pb"] % len(pbl)]
                cnt["pb"] += 1
                for k in range(8):
                    T.op("pe", lambda e, b=b, k=k, n=n, sl=sl: e.matmul(bank(b), lhsT=wbf[sl][:, k, :], rhs=hT[:, k, 512 * n:512 * (n + 1)], start=(k == 0), stop=(k == 7)),
                         reads=[("wbf", sl), "hT"], writes=[("ps", b)], sig=(k == 7))
                    if k % int(_K("STEPK", 2)) == int(_K("STEPK", 2)) - 1 and k < 7:
                        yield
                d = dst[:, 512 * n:512 * (n + 1)]
                src = bank(b)
                if kind == "cast":
                    cnt["evac"] += 1
                    if cnt["evac"] % 2 == 0:
                        T.op("act", lambda e, d=d, src=src: e.activation(out=d, in_=src, func=AF.Copy), reads=[("ps", b)], writes=[dst_key])
                    else:
                        T.op("dve", lambda e, d=d, src=src: e.tensor_copy(out=d, in_=src), reads=[("ps", b)], writes=[dst_key])
                elif kind in ("silu2", "silu2_mul"):
                    ti = cnt["tmp"] % 4
                    cnt["tmp"] += 1
                    t1 = tmp[ti]
                    T.op("act", lambda e, t1=t1, src=src: e.activation(out=t1[:], in_=src, func=AF.Tanh, scale=0.5), reads=[("ps", b)], writes=[("tmp", ti)])
                    if kind == "silu2":
                        T.op("dve", lambda e, t1=t1, src=src, d=d: e.scalar_tensor_tensor(out=d, in0=t1[:], scalar=1.0, in1=src, op0=ALU.add, op1=ALU.mult),
                             reads=[("ps", b), ("tmp", ti)], writes=[dst_key])
                    else:
                        T.op("dve", lambda e, t1=t1, src=src: e.scalar_tensor_tensor(out=t1[:], in0=t1[:], scalar=1.0, in1=src, op0=ALU.add, op1=ALU.mult),
                             reads=[("ps", b), ("tmp", ti)], writes=[("tmp", ti)])
                        T.op("pool", lambda e, t1=t1, d=d: e.tensor_tensor(out=d, in0=d, in1=t1[:], op=ALU.mult),
                             reads=[("tmp", ti), dst_key], writes=[dst_key])
                elif kind == "gelu2":
                    ti = cnt["tmp"] % 4
                    tj = (cnt["tmp"] + 1) % 4
                    cnt["tmp"] += 2
                    t1, t2 = tmp[ti], tmp[tj]
                    T.op("act", lambda e, t1=t1, src=src: e.activation(out=t1[:], in_=src, func=AF.Square, scale=0.044715 ** 0.5), reads=[("ps", b)], writes=[("tmp", ti)])
                    T.op("dve", lambda e, t1=t1, t2=t2, src=src: e.scalar_tensor_tensor(out=t2[:], in0=t1[:], scalar=1.0, in1=src, op0=ALU.add, op1=ALU.mult),
                         reads=[("ps", b), ("tmp", ti)], writes=[("tmp", tj)])
                    T.op("act", lambda e, t2=t2: e.activation(out=t2[:], in_=t2[:], func=AF.Tanh, scale=GELU_C), reads=[("tmp", tj)], writes=[("tmp", tj)])
                    T.op("dve", lambda e, t2=t2, src=src, d=d: e.scalar_tensor_tensor(out=d, in0=t2[:], scalar=1.0, in1=src, op0=ALU.add, op1=ALU.mult),
                         reads=[("ps", b), ("tmp", tj)], writes=[dst_key])
                yield

        def interleave(main, side, q):
            acc = 0.0
            for _ in main:
                acc += q
                while side is not None and acc >= 1.0:
                    acc -= 1.0
                    next(side, None)
            if side is not None:
                for _ in side:
                    pass

        def norm_transpose(src_ap_fn, ntiles, use_g, dstT, dst_key, sskey, ss, lnv, rstd):
            dkeys = dst_key if isinstance(dst_key, list) else [dst_key]
            PRE = 3

            def slot(t):
                q = t % 4
                return q, q // 2, xin[q // 2][:, 1024 * (q % 2):1024 * (q % 2) + D]

            def stA(t):
                q, sl, xv = slot(t)
                T.dma("sp", out=xv, in_=src_ap_fn(t), writes=([("xin", sl), ("xq", q)] if t < 4 else [("xq", q)]), semkey=f"xq{q}")
                T.op("act", lambda e, xv=xv, t=t: e.activation(out=junk[:, :], in_=xv, func=AF.Square, accum_out=ss[:, t:t + 1]),
                     reads=[("xq", q)], writes=[("tmp", 0), (sskey, t)])
                T.op("act", lambda e, t=t: e.activation(out=lnv[:, t:t + 1], in_=ss[:, t:t + 1], func=AF.Ln, bias=EPS, scale=1.0 / D),
                     reads=[(sskey, t)], writes=[(sskey + "l", t)])
                T.op("act", lambda e, t=t: e.activation(out=rstd[:, t:t + 1], in_=lnv[:, t:t + 1], func=AF.Exp, scale=-0.5),
                     reads=[(sskey + "l", t)], writes=[(sskey + "r", t)])

            def stB(t):
                q, sl, xv = slot(t)
                hs = t % 2
                hb = work[hs][:, 0:D]
                rk = [("xq", q), (sskey + "r", t)] + ([("xin", sl)] if t >= ntiles - 4 else [])
                if use_g:
                    T.op("dve", lambda e, hb=hb, xv=xv, t=t: e.scalar_tensor_tensor(out=hb, in0=xv, scalar=rstd[:, t:t + 1], in1=gn_bc[:], op0=ALU.mult, op1=ALU.mult),
                         reads=rk + ["gn_bc"], writes=[("work", hs)])
                else:
                    T.op("dve", lambda e, hb=hb, xv=xv, t=t: e.tensor_scalar(out=hb, in0=xv, scalar1=rstd[:, t:t + 1], scalar2=None, op0=ALU.mult),
                         reads=rk, writes=[("work", hs)])
                tb = 2 + t % 6
                for k in range(8):
                    T.op("pe", lambda e, tb=tb, k=k, hb=hb: e.transpose(out=bankb(tb)[:, 128 * k:128 * (k + 1)], in_=hb[:, k:D:8], identity=identb[:]),
                         reads=[("work", hs), "identb"], writes=[("ps", tb)], sig=(k == 7))

            def stC(t):
                tb = 2 + t % 6
                if t % 4 == 0:
                    T.op("act", lambda e, tb=tb, t=t: e.activation(out=dstT[:, :, 128 * t:128 * (t + 1)], in_=bankb(tb).rearrange("p (k t) -> p k t", k=8), func=AF.Copy),
                         reads=[("ps", tb)], writes=dkeys)
                else:
                    T.op("dve", lambda e, tb=tb, t=t: e.tensor_copy(out=dstT[:, :, 128 * t:128 * (t + 1)], in_=bankb(tb).rearrange("p (k t) -> p k t", k=8)),
                         reads=[("ps", tb)], writes=dkeys)

            for t in range(min(PRE, ntiles)):
                stA(t)
            for t in range(ntiles):
                if t + PRE < ntiles:
                    stA(t + PRE)
                stB(t)
                if t > 0:
                    stC(t - 1)
            stC(ntiles - 1)

        rkeys = [("rL", i) for i in range(4)]
        memT = rL.bitcast(BF16)[:, 0:8 * NMEM].rearrange("p (k m) -> p k m", k=8)

        def P_B(W):
            for j in range(2):
                yield from gen_proj("gelu2", W[j][:, :], ("work", W[j].wkey))
            for j in range(2):
                yield from gen_proj("gelu2", W[2 + j][:, :], ("work", W[2 + j].wkey))
            for j in range(2):
                yield from gen_proj("silu2_mul", W[2 + j][:, :], ("work", W[2 + j].wkey))

        def Mx_B(W):
            for q4 in range(4):
                tb = 2 + q4 % 3
                for t4 in range(4):
                    tt = 4 * q4 + t4
                    for j in range(2):
                        T.op("pe", lambda e, tb=tb, t4=t4, j=j, tt=tt: e.transpose(out=bankb(tb)[:, 256 * t4 + 128 * j:256 * t4 + 128 * (j + 1)], in_=W[j][:, 128 * tt:128 * (tt + 1)], identity=identb[:]),
                             reads=[("work", W[j].wkey), "identb"], writes=[("ps", tb)], sig=(t4 == 3 and j == 1))
                T.op("dve", lambda e, tb=tb, q4=q4: e.tensor_copy(out=vraw_v[:, 4 * q4:4 * q4 + 4, :], in_=bankb(tb).rearrange("p (t c) -> p t c", c=256)),
                     reads=[("ps", tb)], writes=[("xin", 0)])
                for t4 in range(4):
                    tt = 4 * q4 + t4
                    T.op("act", lambda e, tt=tt: e.activation(out=junk[:, 0:256], in_=vraw_v[:, tt, :], func=AF.Square, accum_out=ss2[:, tt:tt + 1]),
                         reads=[("xin", 0)], writes=[("tmp", 0), "ss2"])
                yield
            T.op("act", lambda e: e.activation(out=lnv2[:], in_=ss2[:], func=AF.Ln, bias=4.0 * EPS, scale=1.0 / 256.0), reads=["ss2"], writes=["lnv2"])
            T.op("act", lambda e: e.activation(out=rstd2[:], in_=lnv2[:], func=AF.Exp, scale=-0.5), reads=["lnv2"], writes=["rstd2"])
            for tt in range(16):
                T.op("dve", lambda e, tt=tt: e.tensor_scalar(out=vraw_v[:, tt, :], in0=vraw_v[:, tt, :], scalar1=rstd2[:, tt:tt + 1], scalar2=None, op0=ALU.mult),
                     reads=[("xin", 0), "rstd2"], writes=[("xin", 0)])
                if tt % 4 == 3:
                    yield
            for n in range(4):
                for j in range(2):
                    pb = 5 + j
                    for q in range(4):
                        tt = 4 * n + q
                        for gg in range(2):
                            g = 2 * j + gg
                            T.op("pe", lambda e, pb=pb, q=q, gg=gg, g=g, tt=tt: e.matmul(bank(pb)[64 * gg:64 * gg + 64, 128 * q:128 * (q + 1)], lhsT=vraw_v[:, tt, 64 * g:64 * (g + 1)], rhs=WsT[:, g, :], start=True, stop=True),
                                 reads=[("xin", 0), "WsT"], writes=[("ps", pb)], sig=(q == 3 and gg == 1))
                    ti = cnt["tmp"] % 4
                    cnt["tmp"] += 1
                    t1 = tmp[ti]
                    bs_bc = bass.AP(bsT, j * 128, [[256, 128], [0, 4], [1, 128]])
                    T.op("dve", lambda e, pb=pb, t1=t1, j=j, bs_bc=bs_bc: e.scalar_tensor_tensor(out=t1[:].rearrange("p (q t) -> p q t", q=4), in0=bank(pb).rearrange("p (q t) -> p q t", q=4), scalar=gvT[:, j:j + 1], in1=bs_bc, op0=ALU.mult, op1=ALU.add),
                         reads=[("ps", pb), "gvT", "bsT"], writes=[("tmp", ti)])
                    T.op("pool", lambda e, t1=t1, j=j, n=n: e.tensor_tensor(out=gatedT[:, 4 + j, 512 * n:512 * (n + 1)], in0=t1[:], in1=W[2 + j][:, 512 * n:512 * (n + 1)], op=ALU.mult),
                         reads=[("tmp", ti), ("work", W[2 + j].wkey)], writes=[("gatedT", 4 + j)])
                    yield

        def P_M(W):
            for j in range(2):
                yield from gen_proj("cast", W[j][:, :], ("work", W[j].wkey))
            for j in range(2):
                yield from gen_proj("silu2", W[2 + j][:, :], ("work", W[2 + j].wkey))

        def Mx_M(W):
            munits = [(j, n, i, hh) for j in range(2) for n in range(4) for i in range(2) for hh in range(2)]
            MB = [2, 3, 4] if _K("MB4", 0) == 0 else [2, 3, 4, 7]
            MLOOK = len(MB) - 1

            def m_qk(ui):
                j, n, i, hh = munits[ui]
                sbk = MB[ui % len(MB)]
                T.op("pe", lambda e, sbk=sbk, hh=hh, j=j, i=i, n=n: e.matmul(bank(sbk), lhsT=kmT[64 * hh:64 * hh + 64, j, 128 * i:128 * (i + 1)], rhs=W[j][64 * hh:64 * hh + 64, 512 * n:512 * (n + 1)], start=True, stop=True),
                     reads=["kmT", ("work", W[j].wkey)], writes=[("ps", sbk)])

            def m_exp(ui):
                j, n, i, hh = munits[ui]
                sbk = MB[ui % len(MB)]
                es_ = ui % 4
                T.op("act", lambda e, sbk=sbk, es_=es_: e.activation(out=ebuf[es_][:], in_=bank(sbk), func=AF.Exp, scale=0.125),
                     reads=[("ps", sbk)], writes=[("ebuf", es_)])

            def m_pv(ui):
                j, n, i, hh = munits[ui]
                es_ = ui % 4
                T.op("pe", lambda e, hh=hh, i=i, j=j, es_=es_: e.matmul(bank(5 + hh), lhsT=vmtok[:, i, j, 64 * hh:64 * hh + 128], rhs=ebuf[es_][:], start=(i == 0), stop=(i == 1)),
                     reads=["vmtok", ("ebuf", es_)], writes=[("ps", 5 + hh)])

            def m_norm(j, n, mc):
                rs = (mc // 2) % 4
                ti = cnt["tmp"] % 4
                cnt["tmp"] += 1
                cs = slice(512 * rs, 512 * (rs + 1))
                for hh in range(2):
                    po = slice(64 * hh, 64 * hh + 64)
                    pl = slice(64 - 64 * hh, 128 - 64 * hh)
                    T.op("dve", lambda e, hh=hh, po=po, j=j, n=n: e.tensor_tensor(out=tmp[ti][po, :], in0=bank(5 + hh)[po, :], in1=W[2 + j][po, 512 * n:512 * (n + 1)], op=ALU.mult),
                         reads=[("ps", 5 + hh), ("work", W[2 + j].wkey)], writes=[("tmp", ti)])
                    T.op("act", lambda e, hh=hh, po=po, pl=pl: e.activation(out=rL[po, cs], in_=bank(5 + hh)[pl, :], func=AF.Ln),
                         reads=[("ps", 5 + hh)], writes=[("rL", rs)])
                T.op("act", lambda e: e.activation(out=rL[:, cs], in_=rL[:, cs], func=AF.Exp, scale=-1.0),
                     reads=[("rL", rs)], writes=[("rL", rs)])
                T.op("pool", lambda e, j=j, n=n: e.tensor_tensor(out=gatedT[:, 6 + j, 512 * n:512 * (n + 1)], in0=tmp[ti][:, :], in1=rL[:, cs], op=ALU.mult),
                     reads=[("tmp", ti), ("rL", rs)], writes=[("gatedT", 6 + j)])

            NM = len(munits)
            mcount = 0
            for ui in range(MLOOK):
                m_qk(ui)
            for ui in range(NM):
                if ui + MLOOK < NM:
                    m_qk(ui + MLOOK)
                m_exp(ui)
                yield
                m_pv(ui)
                j, n, i, hh = munits[ui]
                if i == 1 and hh == 1:
                    m_norm(j, n, mcount)
                    mcount += 2

        def P_A(W):
            yield from gen_proj("cast", W[0][:, :], ("work", W[0].wkey))
            yield from gen_proj("cast", W[1][:, :], ("work", W[1].wkey))
            yield from gen_proj("cast", W[2][:, :], ("work", W[2].wkey))
            yield from gen_proj("silu2", W[3][:, :], ("work", W[3].wkey))

        def Mx_A(W, hp):
            qT, kT, vT, zT = W[0], W[1], W[2], W[3]
            kq, kk, kv, kz = [("work", W[i].wkey) for i in range(4)]
            for c in range(3):
                d = DILS[c]
                npr = 16 // d
                for q4 in range(4):
                    vb = (2, 3, 4, 7)[q4]
                    for t4 in range(4):
                        tau = 4 * q4 + t4
                        r, jj = tau // npr, tau % npr
                        t0 = d * (128 * jj) + r
                        T.op("pe", lambda e, t4=t4, t0=t0, d=d, vb=vb: e.transpose(out=bankb(vb)[:, 128 * t4:128 * (t4 + 1)], in_=vT[:, t0:t0 + d * 127 + 1:d], identity=identb[:]),
                             reads=[kv, "identb"], writes=[("ps", vb)], sig=(t4 == 3))
                    o_ap = vtok[:, c * 16 + 4 * q4:c * 16 + 4 * q4 + 4, :].rearrange("p t (c e) -> p t c e", c=3)[:, :, 0:3:2, :]
                    i_ap = bankb(vb)[:, 0:512].rearrange("p (t c e) -> p t c e", t=4, c=2)
                    if q4 % 2 == 0:
                        T.op("dve", lambda e, o_ap=o_ap, i_ap=i_ap: e.tensor_copy(out=o_ap, in_=i_ap), reads=[("ps", vb)], writes=["vtok"])
                    else:
                        T.op("act", lambda e, o_ap=o_ap, i_ap=i_ap: e.activation(out=o_ap, in_=i_ap, func=AF.Copy), reads=[("ps", vb)], writes=["vtok"])
                    yield
            units = []
            for c in range(2):
                d = DILS[c]
                npr = 16 // d
                for blk in range(4):
                    ulist = []
                    if c == 0:
                        segs = [(0, d, 0, 512 * blk, 512 * blk + 512, 0, 1)]
                    else:
                        segs = [(1, 4, blk, 0, 512, 0, 1)] + [(2, 16, blk + 4 * j, 0, 128, j, 4) for j in range(4)]
                    for (cc, dd, r, l0, l1, aoff, cstep) in segs:
                        nprr = 16 // dd
                        for jj in range(nprr):
                            qlo = max(l0, 128 * jj - 64)
                            qhi = min(l1, 128 * jj + 192)
                            if qhi <= qlo:
                                continue
                            ulist.append(dict(c=cc, d=dd, r=r, jj=jj, qlo=qlo, n=qhi - qlo, acol=aoff + cstep * (qlo - l0), cstep=cstep,
                                              moff=qlo - (128 * jj - 64), tau=cc * 16 + r * nprr + jj, blk=blk, ec=c))
                    for ui, u in enumerate(ulist):
                        u["first"] = (ui == 0)
                        u["last"] = (ui == len(ulist) - 1)
                    units += ulist

            ST_BANKS = [(2, 3), (4, 7)]
            NST = len(ST_BANKS)
            NEB = 4

            def emit_qk_h(ui, hh):
                u = units[ui]
                sbk = ST_BANKS[ui % NST][hh]
                d, r, n = u["d"], u["r"], u["n"]
                k0 = d * (128 * u["jj"]) + r
                q0 = d * u["qlo"] + r
                T.op("pe", lambda e, sbk=sbk, hh=hh, k0=k0, q0=q0, d=d, n=n: e.matmul(bank(sbk)[:, 0:n], lhsT=kT[64 * hh:64 * hh + 64, k0:k0 + d * 127 + 1:d], rhs=qT[64 * hh:64 * hh + 64, q0:q0 + d * (n - 1) + 1:d], start=True, stop=True),
                     reads=[kq, kk], writes=[("ps", sbk)])

            def emit_softmax(ui):
                u = units[ui]
                bks = ST_BANKS[ui % NST]
                es_ = ui % NEB
                n = u["n"]
                ev = ebuf[es_][:].rearrange("p (h q) -> p h q", h=2)[:, :, 0:n]
                sv = bass.AP(PS, 512 * bks[0], [[4096, 128], [512 * (bks[1] - bks[0]), 2], [1, n]])
                mv = masks[:, u["c"] * 4 + hp, :, u["moff"]:u["moff"] + n]
                T.op("act", lambda e, ev=ev, sv=sv: e.activation(out=ev, in_=sv, func=AF.Exp, scale=0.125),
                     reads=[("ps", bks[0]), ("ps", bks[1])], writes=[("ebuf", es_)])
                T.op("dve", lambda e, ev=ev, mv=mv: e.tensor_tensor(out=ev, in0=ev, in1=mv, op=ALU.mult),
                     reads=[("ebuf", es_), "masks"], writes=[("ebuf", es_)])

            def emit_pv_h(ui, hh):
                u = units[ui]
                es_ = ui % NEB
                n = u["n"]
                T.op("pe", lambda e, hh=hh, u=u, es_=es_, n=n: e.matmul(bank(5 + hh)[:, u["acol"]:u["acol"] + u["cstep"] * (n - 1) + 1:u["cstep"]], lhsT=vtok[:, u["tau"], 64 * hh:64 * hh + 128], rhs=ebuf[es_][:, 256 * hh:256 * hh + n], start=u["first"], stop=u["last"], skip_group_check=True),
                     reads=["vtok", ("ebuf", es_)], writes=[("ps", 5 + hh)])

            def emit_evac(ui):
                u = units[ui]
                if u["last"]:
                    c, blk = u["ec"], u["blk"]
                    for hh in range(2):
                        oa = xin[hh]
                        if c == 0:
                            dst = oa[:, 512 * blk:512 * (blk + 1)]
                            src = bank(5 + hh)
                            if hh == 0:
                                T.op("act", lambda e, dst=dst, src=src: e.activation(out=dst, in_=src, func=AF.Copy), reads=[("ps", 5 + hh)], writes=[("xin", hh)])
                            else:
                                T.op("dve", lambda e, dst=dst, src=src: e.tensor_copy(out=dst, in_=src), reads=[("ps", 5 + hh)], writes=[("xin", hh)])
                        else:
                            if c == 1:
                                dst = oa[:].rearrange("p (l r) -> p l r", r=4)[:, :, blk]
                                src = bank(5 + hh)
                            else:
                                dst = oa[:].rearrange("p (l r) -> p r l", r=16)[:, 4 * blk:4 * blk + 4, :]
                                src = bank(5 + hh).rearrange("p (r l) -> p r l", r=4)
                            if hh == 0:
                                T.op("dve", lambda e, dst=dst, src=src: e.tensor_tensor(out=dst, in0=src, in1=dst, op=ALU.add),
                                     reads=[("ps", 5 + hh), ("xin", hh)], writes=[("xin", hh)])
                            else:
                                ti = cnt["tmp"] % 4
                                cnt["tmp"] += 1
                                tv = tmp[ti][:] if c == 1 else tmp[ti][:].rearrange("p (r l) -> p r l", r=4)
                                T.op("act", lambda e, tv=tv, src=src: e.activation(out=tv, in_=src, func=AF.Copy),
                                     reads=[("ps", 5 + hh)], writes=[("tmp", ti)])
                                deferred.append(lambda dst=dst, tv=tv, ti=ti, hh=hh: T.op("dve", lambda e: e.tensor_tensor(out=dst, in0=tv, in1=dst, op=ALU.add),
                                                reads=[("tmp", ti), ("xin", hh)], writes=[("xin", hh)]))

            NU = len(units)
            LOOK = NST - 1
            deferred = []
            for hh in range(2):
                for ui in range(min(LOOK, NU)):
                    emit_qk_h(ui, hh)
            for ui in range(NU):
                if ui + LOOK < NU:
                    emit_qk_h(ui + LOOK, 0)
                    emit_qk_h(ui + LOOK, 1)
                emit_softmax(ui)
                while deferred:
                    deferred.pop(0)()
                yield
                emit_pv_h(ui, 0)
                emit_pv_h(ui, 1)
                emit_evac(ui)
            while deferred:
                deferred.pop(0)()
            for hh in range(2):
                po = slice(64 * hh, 64 * hh + 64)
                pl = slice(64 - 64 * hh, 128 - 64 * hh)
                T.op("act", lambda e, hh=hh, po=po, pl=pl: e.activation(out=rL[po, :], in_=xin[hh][pl, :], func=AF.Ln),
                     reads=[("xin", hh)], writes=rkeys)
            T.op("act", lambda e: e.activation(out=rL[:, :], in_=rL[:, :], func=AF.Exp, scale=-1.0),
                 reads=rkeys, writes=rkeys)
            yield
            T.op("pool", lambda e: e.tensor_tensor(out=rL[:, :], in0=rL[:, :], in1=zT[:, :], op=ALU.mult),
                 reads=rkeys + [kz], writes=rkeys)
            for hh in range(2):
                po = slice(64 * hh, 64 * hh + 64)
                T.op("pool", lambda e, hh=hh, po=po: e.tensor_tensor(out=gatedT[po, hp, :], in0=xin[hh][po, :], in1=rL[po, :], op=ALU.mult),
                     reads=[("xin", hh)] + rkeys, writes=[("gatedT", hp)])
            yield

        class WS:
            def __init__(self, t, k):
                self.t = t
                self.wkey = k

            def __getitem__(self, idx):
                return self.t[idx]

        WSETS = [[WS(work[4 * s_ + i], 4 * s_ + i) for i in range(4)] for s_ in range(2)]

        setup_early()
        for b in range(NSEQ):
            xb = b * SEQ
            norm_transpose(lambda t: x_d[xb + 128 * t:xb + 128 * (t + 1), :],
                           16, True, hT, "hT", "ssx", ss, lnv, rstd)
            if b == 0:
                setup_mid()
            norm_transpose(lambda t: mem_d[b * NMEM + 128 * t:b * NMEM + 128 * (t + 1), :],
                           2, False, memT, rkeys, "ssm", ssm, lnvm, rstdm)
            for j in range(2):
                for k in range(8):
                    T.op("pe", lambda e, j=j, k=k: e.matmul(bank(j)[:, 0:NMEM], lhsT=wkv[:, k, 128 * j:128 * (j + 1)], rhs=memT[:, k, :], start=(k == 0), stop=(k == 7)),
                         reads=["wkv"] + rkeys, writes=[("ps", j)], sig=(k == 7))
                T.op("dve", lambda e, j=j: e.tensor_copy(out=kmT[:, j, :], in_=bank(j)[:, 0:NMEM]), reads=[("ps", j)], writes=["kmT"])
            for i in range(2):
                for k in range(8):
                    T.op("pe", lambda e, i=i, k=k: e.matmul(bank(i)[:, 0:256], lhsT=memT[:, k, 128 * i:128 * (i + 1)], rhs=wkv[:, k, 256:512], start=(k == 0), stop=(k == 7)),
                         reads=["wkv"] + rkeys, writes=[("ps", i)], sig=(k == 7))
                o_ap = vmtok[:, i, :, :].rearrange("p j (c e) -> p j c e", c=3)[:, :, 0:3:2, :]
                i_ap = bank(i)[:, 0:256].rearrange("p (j c e) -> p j c e", j=2, c=2)
                T.op("dve", lambda e, o_ap=o_ap, i_ap=i_ap: e.tensor_copy(out=o_ap, in_=i_ap), reads=[("ps", i)], writes=["vmtok"])

            jobs = [("A", hp) for hp in range(4)] + [("M", None), ("B", None)]

            def mkP(ji):
                kind, hp = jobs[ji]
                W = WSETS[ji % 2]
                return {"B": P_B, "M": P_M, "A": P_A}[kind](W)

            def mkMx(ji):
                kind, hp = jobs[ji]
                W = WSETS[ji % 2]
                if kind == "B":
                    return Mx_B(W), 4.0
                if kind == "M":
                    return Mx_M(W), 2.0
                return Mx_A(W, hp), _K("RA", 1.0)

            interleave(mkP(0), setup_late() if b == 0 else None, 0.55)
            for ji in range(len(jobs)):
                main, R = mkMx(ji)
                side = mkP(ji + 1) if ji + 1 < len(jobs) else None
                if side is not None and jobs[ji + 1][0] == "B":
                    R = _K("RM", 3.0)
                    cnt["pbanks"] = (0, 1, 7) if _K("MB4", 0) == 0 else (0, 1)
                interleave(main, side, R)
                cnt["pbanks"] = (0, 1)

            gkeys = [("gatedT", k) for k in range(8)]
            for tt in range(16):
                q = tt % 4
                sl = q // 2
                xr = xin[sl][:, 1024 * (q % 2):1024 * (q % 2) + D]
                kx = ("xr", q)
                T.dma("sp", out=xr, in_=x_d[xb + 128 * tt:xb + 128 * (tt + 1), :], writes=([("xin", sl), kx] if tt < 4 else [kx]), semkey=f"xr{q}")
                ob = 2 * (tt % 3)
                for half in range(2):
                    for k in range(8):
                        T.op("pe", lambda e, half=half, k=k, tt=tt, ob=ob: e.matmul(bank(ob + half), lhsT=gatedT[:, k, 128 * tt:128 * (tt + 1)], rhs=wout[:, k, 512 * half:512 * (half + 1)], start=(k == 0), stop=(k == 7)),
                             reads=gkeys + ["wout"], writes=[("ps", ob + half)], sig=(k == 7))
                T.op("dve", lambda e, xr=xr, ob=ob: e.tensor_tensor(out=xr, in0=PS[:, 512 * ob:512 * ob + 1024], in1=xr, op=ALU.add),
                     reads=[("ps", ob), ("ps", ob + 1), kx], writes=[kx])
                T.op("act", lambda e, xr=xr, tt=tt: e.activation(out=junk[:, :], in_=xr, func=AF.Square, accum_out=sso[:, tt:tt + 1]),
                     reads=[kx], writes=[("tmp", 0), ("sso", tt)])
                T.op("act", lambda e, tt=tt: e.activation(out=lnvo[:, tt:tt + 1], in_=sso[:, tt:tt + 1], func=AF.Ln, bias=EPS, scale=1.0 / D),
                     reads=[("sso", tt)], writes=[("ssol", tt)])
                T.op("act", lambda e, tt=tt: e.activation(out=rstdo[:, tt:tt + 1], in_=lnvo[:, tt:tt + 1], func=AF.Exp, scale=-0.5),
                     reads=[("ssol", tt)], writes=[("ssor", tt)])
                T.op("dve", lambda e, xr=xr, tt=tt: e.scalar_tensor_tensor(out=xr, in0=xr, scalar=rstdo[:, tt:tt + 1], in1=gfin_bc[:], op0=ALU.mult, op1=ALU.mult),
                     reads=[kx, ("ssor", tt), "gfin_bc"], writes=[kx])
                T.dma("pool", out=out_d[xb + 128 * tt:xb + 128 * (tt + 1), :], in_=xr, reads=([kx, ("xin", sl)] if tt >= 12 else [kx]), semkey=f"out{q}")

        T.finish("pool", ["out0", "out1", "out2", "out3"])
        print("instructions emitted:", T.nins, {k: v[1] for k, v in T.sems.items()})
    return nc


_CACHE = {}


def _consts():
    k = np.arange(128)[:, None]
    q = np.arange(256)[None, :]
    rel = np.abs(q - 64 - k).astype(np.float32)
    band = (rel <= 64).astype(np.float32)
    return np.eye(128, dtype=np.float32), rel, band


def kernel(x, mem, g_norm, w_in, w_sgu_spatial, b_sgu_spatial, g_sgu_v, g_mem, w_mem_kv, w_out, g_final):
    f = lambda a: np.ascontiguousarray(np.asarray(a, dtype=np.float32))
    x = f(x); mem = f(mem)
    if "nc" not in _CACHE:
        _CACHE["nc"] = build_program()
    nc = _CACHE["nc"]
    ident, rel, band = _consts()
    shared = {
        "g_norm": f(g_norm).reshape(D), "w_in": f(w_in).reshape(D, INC), "w_s": f(w_sgu_spatial).reshape(4, 128, 128),
        "b_s": f(b_sgu_spatial).reshape(4, 128), "g_v": f(g_sgu_v).reshape(256), "g_mem": f(g_mem).reshape(D),
        "w_kv": f(w_mem_kv).reshape(D, 512), "w_out": f(w_out).reshape(D, D), "g_final": f(g_final).reshape(D),
        "ident": ident, "relabs": rel, "band": band,
    }
    in_maps = []
    for c in range(NCORES):
        m = dict(shared)
        m["x"] = x[NSEQ * c:NSEQ * (c + 1)].reshape(NSEQ * SEQ, D)
        m["mem"] = mem[NSEQ * c:NSEQ * (c + 1)].reshape(NSEQ * NMEM, D)
        in_maps.append(m)
    res = run_bass_kernel_spmd(nc, in_maps, core_ids=list(range(NCORES)))
    out = np.concatenate([r["out"].reshape(NSEQ, SEQ, D) for r in res.results], axis=0)
    return out.astype(np.float32)
```

```python
import numpy as np
_KN = {"RA": 1.1, "RM": 3.0, "STEPK": 4, "MB4": 0}
_K = lambda n, d: float(_KN.get(n, d))
from contextlib import ExitStack
import concourse.bass as bass
import concourse.mybir as mybir
from concourse.bass_utils import run_bass_kernel_spmd

F32 = mybir.dt.float32
BF16 = mybir.dt.bfloat16
AF = mybir.ActivationFunctionType
ALU = mybir.AluOpType

NCORES = 8
SEQ = 2048
D = 1024
NSEQ = 2
INC = 3328
NMEM = 256
EPS = 1e-6
GELU_C = 0.7978845608028654
DILS = (1, 4, 16)


class Tracker:
    def __init__(self, nc, es):
        self.nc = nc
        self.es = es
        self.eng = {"pe": nc.tensor, "act": nc.scalar, "dve": nc.vector, "pool": nc.gpsimd, "sp": nc.sync}
        self.sems = {}
        self.waited = {}
        self.lastw = {}
        self.readers = {}
        self.nins = 0

    def _sem(self, name):
        if name not in self.sems:
            self.sems[name] = [self.es.enter_context(self.nc.semaphore(name.replace(":", "_"))), 0]
        return self.sems[name]

    def _wait(self, engname, ticket):
        semname, val = ticket
        if semname == engname and engname == "pe":
            return
        cur = self.sems[semname][1]
        assert val <= cur, ("wait on unsignalled ticket", engname, ticket, cur)
        k = (engname, semname)
        if self.waited.get(k, 0) >= val:
            return
        self.eng[engname].wait_ge(self.sems[semname][0], val)
        self.waited[k] = val
        self.nins += 1

    def _deps(self, engname, reads, writes, me):
        for key in reads:
            t = self.lastw.get(key)
            if t is not None:
                self._wait(engname, t)
        for key in writes:
            t = self.lastw.get(key)
            if t is not None and t[0] != me:
                self._wait(engname, t)
            for sn, v in self.readers.get(key, {}).items():
                if sn != me:
                    self._wait(engname, (sn, v))

    def _record(self, ticket, reads, writes):
        for key in reads:
            d = self.readers.setdefault(key, {})
            if d.get(ticket[0], 0) < ticket[1]:
                d[ticket[0]] = ticket[1]
        for key in writes:
            self.lastw[key] = ticket
            self.readers[key] = {}

    def op(self, engname, fn, reads=(), writes=(), sig=True):
        self._deps(engname, reads, writes, engname)
        ins = fn(self.eng[engname])
        s = self._sem(engname)
        if sig:
            s[1] += 1
            ins.then_inc(s[0], 1)
            ticket = (engname, s[1])
        else:
            ticket = (engname, s[1] + 1)
        self._record(ticket, reads, writes)
        self.nins += 1
        return ins

    def dma(self, qname, out, in_, reads=(), writes=(), semkey="d"):
        self._deps(qname, reads, writes, None)
        ins = self.eng[qname].dma_start(out=out, in_=in_)
        s = self._sem("dma:" + semkey)
        s[1] += 16
        ins.then_inc(s[0], 16)
        self._record(("dma:" + semkey, s[1]), reads, writes)
        self.nins += 1
        return ins

    def finish(self, qname, semkeys):
        for sk in semkeys:
            s = self.sems.get("dma:" + sk)
            if s is not None:
                self.eng[qname].wait_ge(s[0], s[1])


def build_program():
    nc = bass.Bass("TRN2", target_bir_lowering=False)
    dt_in = lambda name, shape: nc.dram_tensor(name, shape, F32, kind="ExternalInput").ap()
    x_d = dt_in("x", [NSEQ * SEQ, D])
    mem_d = dt_in("mem", [NSEQ * NMEM, D])
    g_norm_d = dt_in("g_norm", [D])
    w_in_d = dt_in("w_in", [D, INC])
    w_s_d = dt_in("w_s", [4, 128, 128])
    b_s_d = dt_in("b_s", [4, 128])
    g_v_d = dt_in("g_v", [256])
    g_mem_d = dt_in("g_mem", [D])
    w_kv_d = dt_in("w_kv", [D, 512])
    w_out_d = dt_in("w_out", [D, D])
    g_fin_d = dt_in("g_final", [D])
    ident_d = dt_in("ident", [128, 128])
    relabs_d = dt_in("relabs", [128, 256])
    band_d = dt_in("band", [128, 256])
    out_d = nc.dram_tensor("out", [NSEQ * SEQ, D], F32, kind="ExternalOutput").ap()

    with ExitStack() as es:
        T = Tracker(nc, es)

        def sb(name, shape, dt):
            return es.enter_context(nc.sbuf_tensor(name, shape, dt))

        identb = sb("identb", [128, 128], BF16)
        gn_bc = sb("gn_bc", [128, D], F32)
        gfin_bc = sb("gfin_bc", [128, D], F32)
        wout = sb("wout", [128, 8, D], BF16)
        wkv = sb("wkv", [128, 8, 512], BF16)
        WsT = sb("WsT", [128, 4, 128], BF16)
        bsT = sb("bsT", [128, 2, 128], F32)
        gvT = sb("gvT", [128, 2], F32)
        gmT = sb("gmT", [128, 8], F32)
        masks = sb("masks", [128, 12, 2, 256], BF16)
        xin = [sb(f"xin{i}", [128, 2048], F32) for i in range(2)]
        hT = sb("hT", [128, 8, SEQ], BF16)
        gatedT = sb("gatedT", [128, 8, SEQ], BF16)
        work = [sb(f"work{i}", [128, SEQ], BF16) for i in range(8)]
        wbf = [sb(f"wbf{i}", [128, 8, 128], BF16) for i in range(3)]
        vtok = sb("vtok", [128, 48, 192], BF16)
        ebuf = [sb(f"ebuf{i}", [128, 512], BF16) for i in range(4)]
        tmp = [sb(f"tmp{i}", [128, 512], F32) for i in range(4)]
        kmT = sb("kmT", [128, 2, NMEM], BF16)
        vmtok = sb("vmtok", [128, 2, 2, 192], BF16)
        rL = sb("rL", [128, 2048], F32)
        ss = sb("ss", [128, 16], F32)
        lnv = sb("lnv", [128, 16], F32)
        rstd = sb("rstd", [128, 16], F32)
        ss2 = sb("ss2", [128, 16], F32)
        lnv2 = sb("lnv2", [128, 16], F32)
        rstd2 = sb("rstd2", [128, 16], F32)
        ssm = sb("ssm", [128, 2], F32)
        lnvm = sb("lnvm", [128, 2], F32)
        rstdm = sb("rstdm", [128, 2], F32)
        sso = sb("sso", [128, 16], F32)
        lnvo = sb("lnvo", [128, 16], F32)
        rstdo = sb("rstdo", [128, 16], F32)
        PS = es.enter_context(nc.psum_tensor("PS", [128, 4096], F32))
        PSb = PS.bitcast(BF16)

        def bank(i):
            return PS[:, 512 * i:512 * (i + 1)]

        def bankb(i):
            return PSb[:, 1024 * i:1024 * (i + 1)]

        junk = tmp[0].bitcast(BF16)
        vraw_v = xin[0].bitcast(BF16)[:, :].rearrange("p (t c) -> p t c", c=256)

        def setup_early():
            T.dma("sp", out=tmp[0][:, 0:128], in_=ident_d, writes=[("tmp", 0)], semkey="c0")
            T.op("dve", lambda e: e.tensor_copy(out=identb[:], in_=tmp[0][:, 0:128]), reads=[("tmp", 0)], writes=["identb"])
            T.dma("sp", out=gn_bc[:], in_=g_norm_d.partition_broadcast(128), writes=["gn_bc"], semkey="c1")

        def setup_mid():
            T.dma("sp", out=gfin_bc[:], in_=g_fin_d.partition_broadcast(128), writes=["gfin_bc"], semkey="c2")
            T.dma("sp", out=gmT[:], in_=g_mem_d.rearrange("(p k) -> p k", k=8), writes=["gmT"], semkey="c3")
            for j in range(2):
                T.dma("sp", out=gvT[:, j:j + 1], in_=g_v_d[j * 128:(j + 1) * 128].rearrange("(p o) -> p o", o=1),
                      writes=["gvT"], semkey="c4")
            for g in range(4):
                j, gg = g // 2, g % 2
                T.dma("sp", out=bsT[64 * gg:64 * gg + 64, j, :], in_=b_s_d[g, :].partition_broadcast(64),
                      writes=["bsT"], semkey="c5")
            T.op("pool", lambda e: e.memset(vtok[:], 1.0), writes=["vtok"])
            T.op("pool", lambda e: e.memset(vmtok[:], 1.0), writes=["vmtok"])
            T.dma("sp", out=tmp[0][:, 0:512].rearrange("p (g s) -> p g s", g=4), in_=w_s_d.rearrange("g t s -> t g s"),
                  writes=[("tmp", 0)], semkey="c0")
            T.op("dve", lambda e: e.tensor_copy(out=ebuf[0][:], in_=tmp[0][:]), reads=[("tmp", 0)], writes=[("ebuf", 0)])
            for g in range(4):
                T.op("pe", lambda e, g=g: e.transpose(out=bankb(7)[:, 128 * g:128 * (g + 1)], in_=ebuf[0][:, 128 * g:128 * (g + 1)], identity=identb[:]),
                     reads=[("ebuf", 0), "identb"], writes=[("ps", 7)], sig=(g == 3))
            T.op("act", lambda e: e.activation(out=WsT[:].rearrange("p g t -> p (g t)"), in_=bankb(7)[:, 0:512], func=AF.Copy),
                 reads=[("ps", 7)], writes=["WsT"])
            for i in range(2):
                st = xin[i][:].rearrange("p (k n) -> p k n", k=4)
                T.dma("sp", out=st, in_=w_kv_d.rearrange("(p k) n -> p k n", k=8)[:, 4 * i:4 * i + 4, :],
                      writes=[("xin", i)], semkey=f"xin{i}")
                for kk in range(4):
                    k = 4 * i + kk
                    T.op("dve", lambda e, st=st, kk=kk, k=k: e.tensor_scalar(out=wkv[:, k, :], in0=st[:, kk, :], scalar1=gmT[:, k:k + 1], scalar2=None, op0=ALU.mult),
                         reads=[("xin", i), "gmT"], writes=["wkv"])

        def setup_late():
            T.dma("sp", out=rL[:, 0:256], in_=relabs_d, writes=[("rL", 0)], semkey="c6")
            T.dma("sp", out=rL[:, 256:512], in_=band_d, writes=[("rL", 0)], semkey="c6")
            for c in range(3):
                for hp in range(4):
                    for hh in range(2):
                        h = 2 * hp + hh
                        coef = -(2.0 ** (-(h + 1))) * DILS[c]
                        ms = 1 + hh
                        tsl = rL[:, 512 * ms:512 * ms + 256]
                        T.op("act", lambda e, tsl=tsl, coef=coef: e.activation(out=tsl, in_=rL[:, 0:256], func=AF.Exp, scale=coef),
                             reads=[("rL", 0)], writes=[("rL", ms)])
                        T.op("dve", lambda e, tsl=tsl, c=c, hp=hp, hh=hh: e.tensor_tensor(out=masks[:, c * 4 + hp, hh, :], in0=tsl, in1=rL[:, 256:512], op=ALU.mult),
                             reads=[("rL", ms), ("rL", 0)], writes=["masks"])
                    yield
            for i in range(4):
                st = xin[i % 2][:].rearrange("p (k n) -> p k n", k=2)
                T.dma("sp", out=st, in_=w_out_d[256 * i:256 * (i + 1), :].rearrange("(k p) n -> p k n", p=128),
                      writes=[("xin", i % 2)], semkey=f"xin{i % 2}")
                fac = 0.25 if i == 2 else 0.5
                T.op("act", lambda e, st=st, i=i, fac=fac: e.activation(out=wout[:, 2 * i:2 * i + 2, :], in_=st, func=AF.Copy, scale=fac),
                     reads=[("xin", i % 2)], writes=["wout"])
                yield

        chunk_order = []
        for hp in range(4):
            chunk_order += [hp, 4 + hp, 8 + hp, 12 + hp]
        chunk_order += [22, 23, 24, 25]
        chunk_order += [18, 19, 16, 17, 20, 21]
        stream = chunk_order * NSEQ
        wstate = {"loaded": 0}

        def load_chunk(i):
            if i >= len(stream):
                return
            c = stream[i]
            sl = i % 3
            T.dma("pool", out=wbf[sl][:], in_=w_in_d[:, 128 * c:128 * (c + 1)].rearrange("(p k) c -> p k c", k=8),
                  writes=[("wbf", sl)], semkey=f"wbf{sl}")

        def next_weights():
            i = wstate["loaded"]
            if i == 0:
                load_chunk(0)
                load_chunk(1)
            load_chunk(i + 2)
            wstate["loaded"] = i + 1
            return i % 3

        cnt = {"tmp": 0, "evac": 0, "pb": 0}

        def gen_proj(kind, dst, dst_key):
            sl = next_weights()
            for n in range(4):
                pbl = cnt.get("pbanks", (0, 1))
                b = pbl[cnt["pb"] % len(pbl)]
                cnt["pb"] += 1
                for k in range(8):
                    T.op("pe", lambda e, b=b, k=k, n=n, sl=sl: e.matmul(bank(b), lhsT=wbf[sl][:, k, :], rhs=hT[:, k, 512 * n:512 * (n + 1)], start=(k == 0), stop=(k == 7)),
                         reads=[("wbf", sl), "hT"], writes=[("ps", b)], sig=(k == 7))
                    if k % int(_K("STEPK", 2)) == int(_K("STEPK", 2)) - 1 and k < 7:
                        yield
                d = dst[:, 512 * n:512 * (n + 1)]
                src = bank(b)
                if kind == "cast":
                    cnt["evac"] += 1
                    if cnt["evac"] % 2 == 0:
                        T.op("act", lambda e, d=d, src=src: e.activation(out=d, in_=src, func=AF.Copy), reads=[("ps", b)], writes=[dst_key])
                    else:
                        T.op("dve", lambda e, d=d, src=src: e.tensor_copy(out=d, in_=src), reads=[("ps", b)], writes=[dst_key])
                elif kind in ("silu2", "silu2_mul"):
                    ti = cnt["tmp"] % 4
                    cnt["tmp"] += 1
                    t1 = tmp[ti]
                    T.op("act", lambda e, t1=t1, src=src: e.activation(out=t1[:], in_=src, func=AF.Tanh, scale=0.5), reads=[("ps", b)], writes=[("tmp", ti)])
                    if kind == "silu2":
                        T.op("dve", lambda e, t1=t1, src=src, d=d: e.scalar_tensor_tensor(out=d, in0=t1[:], scalar=1.0, in1=src, op0=ALU.add, op1=ALU.mult),
                             reads=[("ps", b), ("tmp", ti)], writes=[dst_key])
                    else:
                        T.op("dve", lambda e, t1=t1, src=src: e.scalar_tensor_tensor(out=t1[:], in0=t1[:], scalar=1.0, in1=src, op0=ALU.add, op1=ALU.mult),
                             reads=[("ps", b), ("tmp", ti)], writes=[("tmp", ti)])
                        T.op("pool", lambda e, t1=t1, d=d: e.tensor_tensor(out=d, in0=d, in1=t1[:], op=ALU.mult),
                             reads=[("tmp", ti), dst_key], writes=[dst_key])
                elif kind == "gelu2":
                    ti = cnt["tmp"] % 4
                    tj = (cnt["tmp"] + 1) % 4
                    cnt["tmp"] += 2
                    t1, t2 = tmp[ti], tmp[tj]
                    T.op("act", lambda e, t1=t1, src=src: e.activation(out=t1[:], in_=src, func=AF.Square, scale=0.044715 ** 0.5), reads=[("ps", b)], writes=[("tmp", ti)])
                    T.op("dve", lambda e, t1=t1, t2=t2, src=src: e.scalar_tensor_tensor(out=t2[:], in0=t1[:], scalar=1.0, in1=src, op0=ALU.add, op1=ALU.mult),
                         reads=[("ps", b), ("tmp", ti)], writes=[("tmp", tj)])
                    T.op("act", lambda e, t2=t2: e.activation(out=t2[:], in_=t2[:], func=AF.Tanh, scale=GELU_C), reads=[("tmp", tj)], writes=[("tmp", tj)])
                    T.op("dve", lambda e, t2=t2, src=src, d=d: e.scalar_tensor_tensor(out=d, in0=t2[:], scalar=1.0, in1=src, op0=ALU.add, op1=ALU.mult),
                         reads=[("ps", b), ("tmp", tj)], writes=[dst_key])
                yield

        def interleave(main, side, q):
            acc = 0.0
            for _ in main:
                acc += q
                while side is not None and acc >= 1.0:
                    acc -= 1.0
                    next(side, None)
            if side is not None:
                for _ in side:
                    pass

        def norm_transpose(src_ap_fn, ntiles, use_g, dstT, dst_key, sskey, ss, lnv, rstd):
            dkeys = dst_key if isinstance(dst_key, list) else [dst_key]
            PRE = 3

            def slot(t):
                q = t % 4
                return q, q // 2, xin[q // 2][:, 1024 * (q % 2):1024 * (q % 2) + D]

            def stA(t):
                q, sl, xv = slot(t)
                T.dma("sp", out=xv, in_=src_ap_fn(t), writes=([("xin", sl), ("xq", q)] if t < 4 else [("xq", q)]), semkey=f"xq{q}")
                T.op("act", lambda e, xv=xv, t=t: e.activation(out=junk[:, :], in_=xv, func=AF.Square, accum_out=ss[:, t:t + 1]),
                     reads=[("xq", q)], writes=[("tmp", 0), (sskey, t)])
                T.op("act", lambda e, t=t: e.activation(out=lnv[:, t:t + 1], in_=ss[:, t:t + 1], func=AF.Ln, bias=EPS, scale=1.0 / D),
                     reads=[(sskey, t)], writes=[(sskey + "l", t)])
                T.op("act", lambda e, t=t: e.activation(out=rstd[:, t:t + 1], in_=lnv[:, t:t + 1], func=AF.Exp, scale=-0.5),
                     reads=[(sskey + "l", t)], writes=[(sskey + "r", t)])

            def stB(t):
                q, sl, xv = slot(t)
                hs = t % 2
                hb = work[hs][:, 0:D]
                rk = [("xq", q), (sskey + "r", t)] + ([("xin", sl)] if t >= ntiles - 4 else [])
                if use_g:
                    T.op("dve", lambda e, hb=hb, xv=xv, t=t: e.scalar_tensor_tensor(out=hb, in0=xv, scalar=rstd[:, t:t + 1], in1=gn_bc[:], op0=ALU.mult, op1=ALU.mult),
                         reads=rk + ["gn_bc"], writes=[("work", hs)])
                else:
                    T.op("dve", lambda e, hb=hb, xv=xv, t=t: e.tensor_scalar(out=hb, in0=xv, scalar1=rstd[:, t:t + 1], scalar2=None, op0=ALU.mult),
                         reads=rk, writes=[("work", hs)])
                tb = 2 + t % 6
                for k in range(8):
                    T.op("pe", lambda e, tb=tb, k=k, hb=hb: e.transpose(out=bankb(tb)[:, 128 * k:128 * (k + 1)], in_=hb[:, k:D:8], identity=identb[:]),
                         reads=[("work", hs), "identb"], writes=[("ps", tb)], sig=(k == 7))

            def stC(t):
                tb = 2 + t % 6
                if t % 4 == 0:
                    T.op("act", lambda e, tb=tb, t=t: e.activation(out=dstT[:, :, 128 * t:128 * (t + 1)], in_=bankb(tb).rearrange("p (k t) -> p k t", k=8), func=AF.Copy),
                         reads=[("ps", tb)], writes=dkeys)
                else:
                    T.op("dve", lambda e, tb=tb, t=t: e.tensor_copy(out=dstT[:, :, 128 * t:128 * (t + 1)], in_=bankb(tb).rearrange("p (k t) -> p k t", k=8)),
                         reads=[("ps", tb)], writes=dkeys)

            for t in range(min(PRE, ntiles)):
                stA(t)
            for t in range(ntiles):
                if t + PRE < ntiles:
                    stA(t + PRE)
                stB(t)
                if t > 0:
                    stC(t - 1)
            stC(ntiles - 1)

        rkeys = [("rL", i) for i in range(4)]
        memT = rL.bitcast(BF16)[:, 0:8 * NMEM].rearrange("p (k m) -> p k m", k=8)

        def P_B(W):
            for j in range(2):
                yield from gen_proj("gelu2", W[j][:, :], ("work", W[j].wkey))
            for j in range(2):
                yield from gen_proj("gelu2", W[2 + j][:, :], ("work", W[2 + j].wkey))
            for j in range(2):
                yield from gen_proj("silu2_mul", W[2 + j][:, :], ("work", W[2 + j].wkey))

        def Mx_B(W):
            for q4 in range(4):
                tb = 2 + q4 % 3
                for t4 in range(4):
                    tt = 4 * q4 + t4
                    for j in range(2):
                        T.op("pe", lambda e, tb=tb, t4=t4, j=j, tt=tt: e.transpose(out=bankb(tb)[:, 256 * t4 + 128 * j:256 * t4 + 128 * (j + 1)], in_=W[j][:, 128 * tt:128 * (tt + 1)], identity=identb[:]),
                             reads=[("work", W[j].wkey), "identb"], writes=[("ps", tb)], sig=(t4 == 3 and j == 1))
                T.op("dve", lambda e, tb=tb, q4=q4: e.tensor_copy(out=vraw_v[:, 4 * q4:4 * q4 + 4, :], in_=bankb(tb).rearrange("p (t c) -> p t c", c=256)),
                     reads=[("ps", tb)], writes=[("xin", 0)])
                for t4 in range(4):
                    tt = 4 * q4 + t4
                    T.op("act", lambda e, tt=tt: e.activation(out=junk[:, 0:256], in_=vraw_v[:, tt, :], func=AF.Square, accum_out=ss2[:, tt:tt + 1]),
                         reads=[("xin", 0)], writes=[("tmp", 0), "ss2"])
                yield
            T.op("act", lambda e: e.activation(out=lnv2[:], in_=ss2[:], func=AF.Ln, bias=4.0 * EPS, scale=1.0 / 256.0), reads=["ss2"], writes=["lnv2"])
            T.op("act", lambda e: e.activation(out=rstd2[:], in_=lnv2[:], func=AF.Exp, scale=-0.5), reads=["lnv2"], writes=["rstd2"])
            for tt in range(16):
                T.op("dve", lambda e, tt=tt: e.tensor_scalar(out=vraw_v[:, tt, :], in0=vraw_v[:, tt, :], scalar1=rstd2[:, tt:tt + 1], scalar2=None, op0=ALU.mult),
                     reads=[("xin", 0), "rstd2"], writes=[("xin", 0)])
                if tt % 4 == 3:
                    yield
            for n in range(4):
                for j in range(2):
                    pb = 5 + j
                    for q in range(4):
                        tt = 4 * n + q
                        for gg in range(2):
                            g = 2 * j + gg
                            T.op("pe", lambda e, pb=pb, q=q, gg=gg, g=g, tt=tt: e.matmul(bank(pb)[64 * gg:64 * gg + 64, 128 * q:128 * (q + 1)], lhsT=vraw_v[:, tt, 64 * g:64 * (g + 1)], rhs=WsT[:, g, :], start=True, stop=True),
                                 reads=[("xin", 0), "WsT"], writes=[("ps", pb)], sig=(q == 3 and gg == 1))
                    ti = cnt["tmp"] % 4
                    cnt["tmp"] += 1
                    t1 = tmp[ti]
                    bs_bc = bass.AP(bsT, j * 128, [[256, 128], [0, 4], [1, 128]])
                    T.op("dve", lambda e, pb=pb, t1=t1, j=j, bs_bc=bs_bc: e.scalar_tensor_tensor(out=t1[:].rearrange("p (q t) -> p q t", q=4), in0=bank(pb).rearrange("p (q t) -> p q t", q=4), scalar=gvT[:, j:j + 1], in1=bs_bc, op0=ALU.mult, op1=ALU.add),
                         reads=[("ps", pb), "gvT", "bsT"], writes=[("tmp", ti)])
                    T.op("pool", lambda e, t1=t1, j=j, n=n: e.tensor_tensor(out=gatedT[:, 4 + j, 512 * n:512 * (n + 1)], in0=t1[:], in1=W[2 + j][:, 512 * n:512 * (n + 1)], op=ALU.mult),
                         reads=[("tmp", ti), ("work", W[2 + j].wkey)], writes=[("gatedT", 4 + j)])
                    yield

        def P_M(W):
            for j in range(2):
                yield from gen_proj("cast", W[j][:, :], ("work", W[j].wkey))
            for j in range(2):
                yield from gen_proj("silu2", W[2 + j][:, :], ("work", W[2 + j].wkey))

        def Mx_M(W):
            munits = [(j, n, i, hh) for j in range(2) for n in range(4) for i in range(2) for hh in range(2)]
            MB = [2, 3, 4] if _K("MB4", 0) == 0 else [2, 3, 4, 7]
            MLOOK = len(MB) - 1

            def m_qk(ui):
                j, n, i, hh = munits[ui]
                sbk = MB[ui % len(MB)]
                T.op("pe", lambda e, sbk=sbk, hh=hh, j=j, i=i, n=n: e.matmul(bank(sbk), lhsT=kmT[64 * hh:64 * hh + 64, j, 128 * i:128 * (i + 1)], rhs=W[j][64 * hh:64 * hh + 64, 512 * n:512 * (n + 1)], start=True, stop=True),
                     reads=["kmT", ("work", W[j].wkey)], writes=[("ps", sbk)])

            def m_exp(ui):
                j, n, i, hh = munits[ui]
                sbk = MB[ui % len(MB)]
                es_ = ui % 4
                T.op("act", lambda e, sbk=sbk, es_=es_: e.activation(out=ebuf[es_][:], in_=bank(sbk), func=AF.Exp, scale=0.125),
                     reads=[("ps", sbk)], writes=[("ebuf", es_)])

            def m_pv(ui):
                j, n, i, hh = munits[ui]
                es_ = ui % 4
                T.op("pe", lambda e, hh=hh, i=i, j=j, es_=es_: e.matmul(bank(5 + hh), lhsT=vmtok[:, i, j, 64 * hh:64 * hh + 128], rhs=ebuf[es_][:], start=(i == 0), stop=(i == 1)),
                     reads=["vmtok", ("ebuf", es_)], writes=[("ps", 5 + hh)])

            def m_norm(j, n, mc):
                rs = (mc // 2) % 4
                ti = cnt["tmp"] % 4
                cnt["tmp"] += 1
                cs = slice(512 * rs, 512 * (rs + 1))
                for hh in range(2):
                    po = slice(64 * hh, 64 * hh + 64)
                    pl = slice(64 - 64 * hh, 128 - 64 * hh)
                    T.op("dve", lambda e, hh=hh, po=po, j=j, n=n: e.tensor_tensor(out=tmp[ti][po, :], in0=bank(5 + hh)[po, :], in1=W[2 + j][po, 512 * n:512 * (n + 1)], op=ALU.mult),
                         reads=[("ps", 5 + hh), ("work", W[2 + j].wkey)], writes=[("tmp", ti)])
                    T.op("act", lambda e, hh=hh, po=po, pl=pl: e.activation(out=rL[po, cs], in_=bank(5 + hh)[pl, :], func=AF.Ln),
                         reads=[("ps", 5 + hh)], writes=[("rL", rs)])
                T.op("act", lambda e: e.activation(out=rL[:, cs], in_=rL[:, cs], func=AF.Exp, scale=-1.0),
                     reads=[("rL", rs)], writes=[("rL", rs)])
                T.op("pool", lambda e, j=j, n=n: e.tensor_tensor(out=gatedT[:, 6 + j, 512 * n:512 * (n + 1)], in0=tmp[ti][:, :], in1=rL[:, cs], op=ALU.mult),
                     reads=[("tmp", ti), ("rL", rs)], writes=[("gatedT", 6 + j)])

            NM = len(munits)
            mcount = 0
            for ui in range(MLOOK):
                m_qk(ui)
            for ui in range(NM):
                if ui + MLOOK < NM:
                    m_qk(ui + MLOOK)
                m_exp(ui)
                yield
                m_pv(ui)
                j, n, i, hh = munits[ui]
                if i == 1 and hh == 1:
                    m_norm(j, n, mcount)
                    mcount += 2

        def P_A(W):
            yield from gen_proj("cast", W[0][:, :], ("work", W[0].wkey))
            yield from gen_proj("cast", W[1][:, :], ("work", W[1].wkey))
            yield from gen_proj("cast", W[2][:, :], ("work", W[2].wkey))
            yield from gen_proj("silu2", W[3][:, :], ("work", W[3].wkey))

        def Mx_A(W, hp):
            qT, kT, vT, zT = W[0], W[1], W[2], W[3]
            kq, kk, kv, kz = [("work", W[i].wkey) for i in range(4)]
            for c in range(3):
                d = DILS[c]
                npr = 16 // d
                for q4 in range(4):
                    vb = (2, 3, 4, 7)[q4]
                    for t4 in range(4):
                        tau = 4 * q4 + t4
                        r, jj = tau // npr, tau % npr
                        t0 = d * (128 * jj) + r
                        T.op("pe", lambda e, t4=t4, t0=t0, d=d, vb=vb: e.transpose(out=bankb(vb)[:, 128 * t4:128 * (t4 + 1)], in_=vT[:, t0:t0 + d * 127 + 1:d], identity=identb[:]),
                             reads=[kv, "identb"], writes=[("ps", vb)], sig=(t4 == 3))
                    o_ap = vtok[:, c * 16 + 4 * q4:c * 16 + 4 * q4 + 4, :].rearrange("p t (c e) -> p t c e", c=3)[:, :, 0:3:2, :]
                    i_ap = bankb(vb)[:, 0:512].rearrange("p (t c e) -> p t c e", t=4, c=2)
                    if q4 % 2 == 0:
                        T.op("dve", lambda e, o_ap=o_ap, i_ap=i_ap: e.tensor_copy(out=o_ap, in_=i_ap), reads=[("ps", vb)], writes=["vtok"])
                    else:
                        T.op("act", lambda e, o_ap=o_ap, i_ap=i_ap: e.activation(out=o_ap, in_=i_ap, func=AF.Copy), reads=[("ps", vb)], writes=["vtok"])
                    yield
            units = []
            for c in range(2):
                d = DILS[c]
                npr = 16 // d
                for blk in range(4):
                    ulist = []
                    if c == 0:
                        segs = [(0, d, 0, 512 * blk, 512 * blk + 512, 0, 1)]
                    else:
                        segs = [(1, 4, blk, 0, 512, 0, 1)] + [(2, 16, blk + 4 * j, 0, 128, j, 4) for j in range(4)]
                    for (cc, dd, r, l0, l1, aoff, cstep) in segs:
                        nprr = 16 // dd
                        for jj in range(nprr):
                            qlo = max(l0, 128 * jj - 64)
                            qhi = min(l1, 128 * jj + 192)
                            if qhi <= qlo:
                                continue
                            ulist.append(dict(c=cc, d=dd, r=r, jj=jj, qlo=qlo, n=qhi - qlo, acol=aoff + cstep * (qlo - l0), cstep=cstep,
                                              moff=qlo - (128 * jj - 64), tau=cc * 16 + r * nprr + jj, blk=blk, ec=c))
                    for ui, u in enumerate(ulist):
                        u["first"] = (ui == 0)
                        u["last"] = (ui == len(ulist) - 1)
                    units += ulist

            ST_BANKS = [(2, 3), (4, 7)]
            NST = len(ST_BANKS)
            NEB = 4

            def emit_qk_h(ui, hh):
                u = units[ui]
                sbk = ST_BANKS[ui % NST][hh]
                d, r, n = u["d"], u["r"], u["n"]
                k0 = d * (128 * u["jj"]) + r
                q0 = d * u["qlo"] + r
                T.op("pe", lambda e, sbk=sbk, hh=hh, k0=k0, q0=q0, d=d, n=n: e.matmul(bank(sbk)[:, 0:n], lhsT=kT[64 * hh:64 * hh + 64, k0:k0 + d * 127 + 1:d], rhs=qT[64 * hh:64 * hh + 64, q0:q0 + d * (n - 1) + 1:d], start=True, stop=True),
                     reads=[kq, kk], writes=[("ps", sbk)])

            def emit_softmax(ui):
                u = units[ui]
                bks = ST_BANKS[ui % NST]
                es_ = ui % NEB
                n = u["n"]
                ev = ebuf[es_][:].rearrange("p (h q) -> p h q", h=2)[:, :, 0:n]
                sv = bass.AP(PS, 512 * bks[0], [[4096, 128], [512 * (bks[1] - bks[0]), 2], [1, n]])
                mv = masks[:, u["c"] * 4 + hp, :, u["moff"]:u["moff"] + n]
                T.op("act", lambda e, ev=ev, sv=sv: e.activation(out=ev, in_=sv, func=AF.Exp, scale=0.125),
                     reads=[("ps", bks[0]), ("ps", bks[1])], writes=[("ebuf", es_)])
                T.op("dve", lambda e, ev=ev, mv=mv: e.tensor_tensor(out=ev, in0=ev, in1=mv, op=ALU.mult),
                     reads=[("ebuf", es_), "masks"], writes=[("ebuf", es_)])

            def emit_pv_h(ui, hh):
                u = units[ui]
                es_ = ui % NEB
                n = u["n"]
                T.op("pe", lambda e, hh=hh, u=u, es_=es_, n=n: e.matmul(bank(5 + hh)[:, u["acol"]:u["acol"] + u["cstep"] * (n - 1) + 1:u["cstep"]], lhsT=vtok[:, u["tau"], 64 * hh:64 * hh + 128], rhs=ebuf[es_][:, 256 * hh:256 * hh + n], start=u["first"], stop=u["last"], skip_group_check=True),
                     reads=["vtok", ("ebuf", es_)], writes=[("ps", 5 + hh)])

            def emit_evac(ui):
                u = units[ui]
                if u["last"]:
                    c, blk = u["ec"], u["blk"]
                    for hh in range(2):
                        oa = xin[hh]
                        if c == 0:
                            dst = oa[:, 512 * blk:512 * (blk + 1)]
                            src = bank(5 + hh)
                            if hh == 0:
                                T.op("act", lambda e, dst=dst, src=src: e.activation(out=dst, in_=src, func=AF.Copy), reads=[("ps", 5 + hh)], writes=[("xin", hh)])
                            else:
                                T.op("dve", lambda e, dst=dst, src=src: e.tensor_copy(out=dst, in_=src), reads=[("ps", 5 + hh)], writes=[("xin", hh)])
                        else:
                            if c == 1:
                                dst = oa[:].rearrange("p (l r) -> p l r", r=4)[:, :, blk]
                                src = bank(5 + hh)
                            else:
                                dst = oa[:].rearrange("p (l r) -> p r l", r=16)[:, 4 * blk:4 * blk + 4, :]
                                src = bank(5 + hh).rearrange("p (r l) -> p r l", r=4)
                            if hh == 0:
                                T.op("dve", lambda e, dst=dst, src=src: e.tensor_tensor(out=dst, in0=src, in1=dst, op=ALU.add),
                                     reads=[("ps", 5 + hh), ("xin", hh)], writes=[("xin", hh)])
                            else:
                                ti = cnt["tmp"] % 4
                                cnt["tmp"] += 1
                                tv = tmp[ti][:] if c == 1 else tmp[ti][:].rearrange("p (r l) -> p r l", r=4)
                                T.op("act", lambda e, tv=tv, src=src: e.activation(out=tv, in_=src, func=AF.Copy),
                                     reads=[("ps", 5 + hh)], writes=[("tmp", ti)])
                                deferred.append(lambda dst=dst, tv=tv, ti=ti, hh=hh: T.op("dve", lambda e: e.tensor_tensor(out=dst, in0=tv, in1=dst, op=ALU.add),
                                                reads=[("tmp", ti), ("xin", hh)], writes=[("xin", hh)]))

            NU = len(units)
            LOOK = NST - 1
            deferred = []
            for hh in range(2):
                for ui in range(min(LOOK, NU)):
                    emit_qk_h(ui, hh)
            for ui in range(NU):
                if ui + LOOK < NU:
                    emit_qk_h(ui + LOOK, 0)
                    emit_qk_h(ui + LOOK, 1)
                emit_softmax(ui)
                while deferred:
                    deferred.pop(0)()
                yield
                emit_pv_h(ui, 0)
                emit_pv_h(ui, 1)
                emit_evac(ui)
            while deferred:
                deferred.pop(0)()
            for hh in range(2):
                po = slice(64 * hh, 64 * hh + 64)
                pl = slice(64 - 64 * hh, 128 - 64 * hh)
                T.op("act", lambda e, hh=hh, po=po, pl=pl: e.activation(out=rL[po, :], in_=xin[hh][pl, :], func=AF.Ln),
                     reads=[("xin", hh)], writes=rkeys)
            T.op("act", lambda e: e.activation(out=rL[:, :], in_=rL[:, :], func=AF.Exp, scale=-1.0),
                 reads=rkeys, writes=rkeys)
            yield
            T.op("pool", lambda e: e.tensor_tensor(out=rL[:, :], in0=rL[:, :], in1=zT[:, :], op=ALU.mult),
                 reads=rkeys + [kz], writes=rkeys)
            for hh in range(2):
                po = slice(64 * hh, 64 * hh + 64)
                T.op("pool", lambda e, hh=hh, po=po: e.tensor_tensor(out=gatedT[po, hp, :], in0=xin[hh][po, :], in1=rL[po, :], op=ALU.mult),
                     reads=[("xin", hh)] + rkeys, writes=[("gatedT", hp)])
            yield

        class WS:
            def __init__(self, t, k):
                self.t = t
                self.wkey = k

            def __getitem__(self, idx):
                return self.t[idx]

        WSETS = [[WS(work[4 * s_ + i], 4 * s_ + i) for i in range(4)] for s_ in range(2)]

        setup_early()
        for b in range(NSEQ):
            xb = b * SEQ
            norm_transpose(lambda t: x_d[xb + 128 * t:xb + 128 * (t + 1), :],
                           16, True, hT, "hT", "ssx", ss, lnv, rstd)
            if b == 0:
                setup_mid()
            norm_transpose(lambda t: mem_d[b * NMEM + 128 * t:b * NMEM + 128 * (t + 1), :],
                           2, False, memT, rkeys, "ssm", ssm, lnvm, rstdm)
            for j in range(2):
                for k in range(8):
                    T.op("pe", lambda e, j=j, k=k: e.matmul(bank(j)[:, 0:NMEM], lhsT=wkv[:, k, 128 * j:128 * (j + 1)], rhs=memT[:, k, :], start=(k == 0), stop=(k == 7)),
                         reads=["wkv"] + rkeys, writes=[("ps", j)], sig=(k == 7))
                T.op("dve", lambda e, j=j: e.tensor_copy(out=kmT[:, j, :], in_=bank(j)[:, 0:NMEM]), reads=[("ps", j)], writes=["kmT"])
            for i in range(2):
                for k in range(8):
                    T.op("pe", lambda e, i=i, k=k: e.matmul(bank(i)[:, 0:256], lhsT=memT[:, k, 128 * i:128 * (i + 1)], rhs=wkv[:, k, 256:512], start=(k == 0), stop=(k == 7)),
                         reads=["wkv"] + rkeys, writes=[("ps", i)], sig=(k == 7))
                o_ap = vmtok[:, i, :, :].rearrange("p j (c e) -> p j c e", c=3)[:, :, 0:3:2, :]
                i_ap = bank(i)[:, 0:256].rearrange("p (j c e) -> p j c e", j=2, c=2)
                T.op("dve", lambda e, o_ap=o_ap, i_ap=i_ap: e.tensor_copy(out=o_ap, in_=i_ap), reads=[("ps", i)], writes=["vmtok"])

            jobs = [("A", hp) for hp in range(4)] + [("M", None), ("B", None)]

            def mkP(ji):
                kind, hp = jobs[ji]
                W = WSETS[ji % 2]
                return {"B": P_B, "M": P_M, "A": P_A}[kind](W)

            def mkMx(ji):
                kind, hp = jobs[ji]
                W = WSETS[ji % 2]
                if kind == "B":
                    return Mx_B(W), 4.0
                if kind == "M":
                    return Mx_M(W), 2.0
                return Mx_A(W, hp), _K("RA", 1.0)

            interleave(mkP(0), setup_late() if b == 0 else None, 0.55)
            for ji in range(len(jobs)):
                main, R = mkMx(ji)
                side = mkP(ji + 1) if ji + 1 < len(jobs) else None
                if side is not None and jobs[ji + 1][0] == "B":
                    R = _K("RM", 3.0)
                    cnt["pbanks"] = (0, 1, 7) if _K("MB4", 0) == 0 else (0, 1)
                interleave(main, side, R)
                cnt["pbanks"] = (0, 1)

            gkeys = [("gatedT", k) for k in range(8)]
            for tt in range(16):
                q = tt % 4
                sl = q // 2
                xr = xin[sl][:, 1024 * (q % 2):1024 * (q % 2) + D]
                kx = ("xr", q)
                T.dma("sp", out=xr, in_=x_d[xb + 128 * tt:xb + 128 * (tt + 1), :], writes=([("xin", sl), kx] if tt < 4 else [kx]), semkey=f"xr{q}")
                ob = 2 * (tt % 3)
                for half in range(2):
                    for k in range(8):
                        T.op("pe", lambda e, half=half, k=k, tt=tt, ob=ob: e.matmul(bank(ob + half), lhsT=gatedT[:, k, 128 * tt:128 * (tt + 1)], rhs=wout[:, k, 512 * half:512 * (half + 1)], start=(k == 0), stop=(k == 7)),
                             reads=gkeys + ["wout"], writes=[("ps", ob + half)], sig=(k == 7))
                T.op("dve", lambda e, xr=xr, ob=ob: e.tensor_tensor(out=xr, in0=PS[:, 512 * ob:512 * ob + 1024], in1=xr, op=ALU.add),
                     reads=[("ps", ob), ("ps", ob + 1), kx], writes=[kx])
                T.op("act", lambda e, xr=xr, tt=tt: e.activation(out=junk[:, :], in_=xr, func=AF.Square, accum_out=sso[:, tt:tt + 1]),
                     reads=[kx], writes=[("tmp", 0), ("sso", tt)])
                T.op("act", lambda e, tt=tt: e.activation(out=lnvo[:, tt:tt + 1], in_=sso[:, tt:tt + 1], func=AF.Ln, bias=EPS, scale=1.0 / D),
                     reads=[("sso", tt)], writes=[("ssol", tt)])
                T.op("act", lambda e, tt=tt: e.activation(out=rstdo[:, tt:tt + 1], in_=lnvo[:, tt:tt + 1], func=AF.Exp, scale=-0.5),
                     reads=[("ssol", tt)], writes=[("ssor", tt)])
                T.op("dve", lambda e, xr=xr, tt=tt: e.scalar_tensor_tensor(out=xr, in0=xr, scalar=rstdo[:, tt:tt + 1], in1=gfin_bc[:], op0=ALU.mult, op1=ALU.mult),
                     reads=[kx, ("ssor", tt), "gfin_bc"], writes=[kx])
                T.dma("pool", out=out_d[xb + 128 * tt:xb + 128 * (tt + 1), :], in_=xr, reads=([kx, ("xin", sl)] if tt >= 12 else [kx]), semkey=f"out{q}")

        T.finish("pool", ["out0", "out1", "out2", "out3"])
        print("instructions emitted:", T.nins, {k: v[1] for k, v in T.sems.items()})
    return nc


_CACHE = {}


def _consts():
    k = np.arange(128)[:, None]
    q = np.arange(256)[None, :]
    rel = np.abs(q - 64 - k).astype(np.float32)
    band = (rel <= 64).astype(np.float32)
    return np.eye(128, dtype=np.float32), rel, band


def kernel(x, mem, g_norm, w_in, w_sgu_spatial, b_sgu_spatial, g_sgu_v, g_mem, w_mem_kv, w_out, g_final):
    f = lambda a: np.ascontiguousarray(np.asarray(a, dtype=np.float32))
    x = f(x); mem = f(mem)
    if "nc" not in _CACHE:
        _CACHE["nc"] = build_program()
    nc = _CACHE["nc"]
    ident, rel, band = _consts()
    shared = {
        "g_norm": f(g_norm).reshape(D), "w_in": f(w_in).reshape(D, INC), "w_s": f(w_sgu_spatial).reshape(4, 128, 128),
        "b_s": f(b_sgu_spatial).reshape(4, 128), "g_v": f(g_sgu_v).reshape(256), "g_mem": f(g_mem).reshape(D),
        "w_kv": f(w_mem_kv).reshape(D, 512), "w_out": f(w_out).reshape(D, D), "g_final": f(g_final).reshape(D),
        "ident": ident, "relabs": rel, "band": band,
    }
    in_maps = []
    for c in range(NCORES):
        m = dict(shared)
        m["x"] = x[NSEQ * c:NSEQ * (c + 1)].reshape(NSEQ * SEQ, D)
        m["mem"] = mem[NSEQ * c:NSEQ * (c + 1)].reshape(NSEQ * NMEM, D)
        in_maps.append(m)
    res = run_bass_kernel_spmd(nc, in_maps, core_ids=list(range(NCORES)))
    out = np.concatenate([r["out"].reshape(NSEQ, SEQ, D) for r in res.results], axis=0)
    return out.astype(np.float32)
```

```python
import numpy as np
_KN = {"RA": 1.0, "RM": 3.0, "STEPK": 4, "MB4": 0}
_K = lambda n, d: float(_KN.get(n, d))
from contextlib import ExitStack
import concourse.bass as bass
import concourse.mybir as mybir
from concourse.bass_utils import run_bass_kernel_spmd

F32 = mybir.dt.float32
BF16 = mybir.dt.bfloat16
AF = mybir.ActivationFunctionType
ALU = mybir.AluOpType

NCORES = 8
SEQ = 2048
D = 1024
NSEQ = 2
INC = 3328
NMEM = 256
EPS = 1e-6
GELU_C = 0.7978845608028654
DILS = (1, 4, 16)


class Tracker:
    def __init__(self, nc, es):
        self.nc = nc
        self.es = es
        self.eng = {"pe": nc.tensor, "act": nc.scalar, "dve": nc.vector, "pool": nc.gpsimd, "sp": nc.sync}
        self.sems = {}
        self.waited = {}
        self.lastw = {}
        self.readers = {}
        self.nins = 0

    def _sem(self, name):
        if name not in self.sems:
            self.sems[name] = [self.es.enter_context(self.nc.semaphore(name.replace(":", "_"))), 0]
        return self.sems[name]

    def _wait(self, engname, ticket):
        semname, val = ticket
        if semname == engname and engname == "pe":
            return
        cur = self.sems[semname][1]
        assert val <= cur, ("wait on unsignalled ticket", engname, ticket, cur)
        k = (engname, semname)
        if self.waited.get(k, 0) >= val:
            return
        self.eng[engname].wait_ge(self.sems[semname][0], val)
        self.waited[k] = val
        self.nins += 1

    def _deps(self, engname, reads, writes, me):
        for key in reads:
            t = self.lastw.get(key)
            if t is not None:
                self._wait(engname, t)
        for key in writes:
            t = self.lastw.get(key)
            if t is not None and t[0] != me:
                self._wait(engname, t)
            for sn, v in self.readers.get(key, {}).items():
                if sn != me:
                    self._wait(engname, (sn, v))

    def _record(self, ticket, reads, writes):
        for key in reads:
            d = self.readers.setdefault(key, {})
            if d.get(ticket[0], 0) < ticket[1]:
                d[ticket[0]] = ticket[1]
        for key in writes:
            self.lastw[key] = ticket
            self.readers[key] = {}

    def op(self, engname, fn, reads=(), writes=(), sig=True):
        self._deps(engname, reads, writes, engname)
        ins = fn(self.eng[engname])
        s = self._sem(engname)
        if sig:
            s[1] += 1
            ins.then_inc(s[0], 1)
            ticket = (engname, s[1])
        else:
            ticket = (engname, s[1] + 1)
        self._record(ticket, reads, writes)
        self.nins += 1
        return ins

    def dma(self, qname, out, in_, reads=(), writes=(), semkey="d"):
        self._deps(qname, reads, writes, None)
        ins = self.eng[qname].dma_start(out=out, in_=in_)
        s = self._sem("dma:" + semkey)
        s[1] += 16
        ins.then_inc(s[0], 16)
        self._record(("dma:" + semkey, s[1]), reads, writes)
        self.nins += 1
        return ins

    def finish(self, qname, semkeys):
        for sk in semkeys:
            s = self.sems.get("dma:" + sk)
            if s is not None:
                self.eng[qname].wait_ge(s[0], s[1])


def build_program():
    nc = bass.Bass("TRN2", target_bir_lowering=False)
    dt_in = lambda name, shape: nc.dram_tensor(name, shape, F32, kind="ExternalInput").ap()
    x_d = dt_in("x", [NSEQ * SEQ, D])
    mem_d = dt_in("mem", [NSEQ * NMEM, D])
    g_norm_d = dt_in("g_norm", [D])
    w_in_d = dt_in("w_in", [D, INC])
    w_s_d = dt_in("w_s", [4, 128, 128])
    b_s_d = dt_in("b_s", [4, 128])
    g_v_d = dt_in("g_v", [256])
    g_mem_d = dt_in("g_mem", [D])
    w_kv_d = dt_in("w_kv", [D, 512])
    w_out_d = dt_in("w_out", [D, D])
    g_fin_d = dt_in("g_final", [D])
    ident_d = dt_in("ident", [128, 128])
    relabs_d = dt_in("relabs", [128, 256])
    band_d = dt_in("band", [128, 256])
    out_d = nc.dram_tensor("out", [NSEQ * SEQ, D], F32, kind="ExternalOutput").ap()

    with ExitStack() as es:
        T = Tracker(nc, es)

        def sb(name, shape, dt):
            return es.enter_context(nc.sbuf_tensor(name, shape, dt))

        identb = sb("identb", [128, 128], BF16)
        gn_bc = sb("gn_bc", [128, D], F32)
        gfin_bc = sb("gfin_bc", [128, D], F32)
        wout = sb("wout", [128, 8, D], BF16)
        wkv = sb("wkv", [128, 8, 512], BF16)
        WsT = sb("WsT", [128, 4, 128], BF16)
        bsT = sb("bsT", [128, 2, 128], F32)
        gvT = sb("gvT", [128, 2], F32)
        gmT = sb("gmT", [128, 8], F32)
        masks = sb("masks", [128, 12, 2, 256], BF16)
        xin = [sb(f"xin{i}", [128, 2048], F32) for i in range(2)]
        hT = sb("hT", [128, 8, SEQ], BF16)
        gatedT = sb("gatedT", [128, 8, SEQ], BF16)
        work = [sb(f"work{i}", [128, SEQ], BF16) for i in range(8)]
        wbf = [sb(f"wbf{i}", [128, 8, 128], BF16) for i in range(3)]
        vtok = sb("vtok", [128, 48, 192], BF16)
        ebuf = [sb(f"ebuf{i}", [128, 512], BF16) for i in range(4)]
        tmp = [sb(f"tmp{i}", [128, 512], F32) for i in range(4)]
        kmT = sb("kmT", [128, 2, NMEM], BF16)
        vmtok = sb("vmtok", [128, 2, 2, 192], BF16)
        rL = sb("rL", [128, 2048], F32)
        ss = sb("ss", [128, 16], F32)
        lnv = sb("lnv", [128, 16], F32)
        rstd = sb("rstd", [128, 16], F32)
        ss2 = sb("ss2", [128, 16], F32)
        lnv2 = sb("lnv2", [128, 16], F32)
        rstd2 = sb("rstd2", [128, 16], F32)
        ssm = sb("ssm", [128, 2], F32)
        lnvm = sb("lnvm", [128, 2], F32)
        rstdm = sb("rstdm", [128, 2], F32)
        sso = sb("sso", [128, 16], F32)
        lnvo = sb("lnvo", [128, 16], F32)
        rstdo = sb("rstdo", [128, 16], F32)
        PS = es.enter_context(nc.psum_tensor("PS", [128, 4096], F32))
        PSb = PS.bitcast(BF16)

        def bank(i):
            return PS[:, 512 * i:512 * (i + 1)]

        def bankb(i):
            return PSb[:, 1024 * i:1024 * (i + 1)]

        junk = tmp[0].bitcast(BF16)
        vraw_v = xin[0].bitcast(BF16)[:, :].rearrange("p (t c) -> p t c", c=256)

        def setup_early():
            T.dma("sp", out=tmp[0][:, 0:128], in_=ident_d, writes=[("tmp", 0)], semkey="c0")
            T.op("dve", lambda e: e.tensor_copy(out=identb[:], in_=tmp[0][:, 0:128]), reads=[("tmp", 0)], writes=["identb"])
            T.dma("sp", out=gn_bc[:], in_=g_norm_d.partition_broadcast(128), writes=["gn_bc"], semkey="c1")

        def setup_mid():
            T.dma("sp", out=gfin_bc[:], in_=g_fin_d.partition_broadcast(128), writes=["gfin_bc"], semkey="c2")
            T.dma("sp", out=gmT[:], in_=g_mem_d.rearrange("(p k) -> p k", k=8), writes=["gmT"], semkey="c3")
            for j in range(2):
                T.dma("sp", out=gvT[:, j:j + 1], in_=g_v_d[j * 128:(j + 1) * 128].rearrange("(p o) -> p o", o=1),
                      writes=["gvT"], semkey="c4")
            for g in range(4):
                j, gg = g // 2, g % 2
                T.dma("sp", out=bsT[64 * gg:64 * gg + 64, j, :], in_=b_s_d[g, :].partition_broadcast(64),
                      writes=["bsT"], semkey="c5")
            T.op("pool", lambda e: e.memset(vtok[:], 1.0), writes=["vtok"])
            T.op("pool", lambda e: e.memset(vmtok[:], 1.0), writes=["vmtok"])
            T.dma("sp", out=tmp[0][:, 0:512].rearrange("p (g s) -> p g s", g=4), in_=w_s_d.rearrange("g t s -> t g s"),
                  writes=[("tmp", 0)], semkey="c0")
            T.op("dve", lambda e: e.tensor_copy(out=ebuf[0][:], in_=tmp[0][:]), reads=[("tmp", 0)], writes=[("ebuf", 0)])
            for g in range(4):
                T.op("pe", lambda e, g=g: e.transpose(out=bankb(7)[:, 128 * g:128 * (g + 1)], in_=ebuf[0][:, 128 * g:128 * (g + 1)], identity=identb[:]),
                     reads=[("ebuf", 0), "identb"], writes=[("ps", 7)], sig=(g == 3))
            T.op("act", lambda e: e.activation(out=WsT[:].rearrange("p g t -> p (g t)"), in_=bankb(7)[:, 0:512], func=AF.Copy),
                 reads=[("ps", 7)], writes=["WsT"])
            for i in range(2):
                st = xin[i][:].rearrange("p (k n) -> p k n", k=4)
                T.dma("sp", out=st, in_=w_kv_d.rearrange("(p k) n -> p k n", k=8)[:, 4 * i:4 * i + 4, :],
                      writes=[("xin", i)], semkey=f"xin{i}")
                for kk in range(4):
                    k = 4 * i + kk
                    T.op("dve", lambda e, st=st, kk=kk, k=k: e.tensor_scalar(out=wkv[:, k, :], in0=st[:, kk, :], scalar1=gmT[:, k:k + 1], scalar2=None, op0=ALU.mult),
                         reads=[("xin", i), "gmT"], writes=["wkv"])

        def setup_late():
            T.dma("sp", out=rL[:, 0:256], in_=relabs_d, writes=[("rL", 0)], semkey="c6")
            T.dma("sp", out=rL[:, 256:512], in_=band_d, writes=[("rL", 0)], semkey="c6")
            for c in range(3):
                for hp in range(4):
                    for hh in range(2):
                        h = 2 * hp + hh
                        coef = -(2.0 ** (-(h + 1))) * DILS[c]
                        ms = 1 + hh
                        tsl = rL[:, 512 * ms:512 * ms + 256]
                        T.op("act", lambda e, tsl=tsl, coef=coef: e.activation(out=tsl, in_=rL[:, 0:256], func=AF.Exp, scale=coef),
                             reads=[("rL", 0)], writes=[("rL", ms)])
                        T.op("dve", lambda e, tsl=tsl, c=c, hp=hp, hh=hh: e.tensor_tensor(out=masks[:, c * 4 + hp, hh, :], in0=tsl, in1=rL[:, 256:512], op=ALU.mult),
                             reads=[("rL", ms), ("rL", 0)], writes=["masks"])
                    yield
            for i in range(4):
                st = xin[i % 2][:].rearrange("p (k n) -> p k n", k=2)
                T.dma("sp", out=st, in_=w_out_d[256 * i:256 * (i + 1), :].rearrange("(k p) n -> p k n", p=128),
                      writes=[("xin", i % 2)], semkey=f"xin{i % 2}")
                fac = 0.25 if i == 2 else 0.5
                T.op("act", lambda e, st=st, i=i, fac=fac: e.activation(out=wout[:, 2 * i:2 * i + 2, :], in_=st, func=AF.Copy, scale=fac),
                     reads=[("xin", i % 2)], writes=["wout"])
                yield

        chunk_order = []
        for hp in range(4):
            chunk_order += [hp, 4 + hp, 8 + hp, 12 + hp]
        chunk_order += [22, 23, 24, 25]
        chunk_order += [18, 19, 16, 17, 20, 21]
        stream = chunk_order * NSEQ
        wstate = {"loaded": 0}

        def load_chunk(i):
            if i >= len(stream):
                return
            c = stream[i]
            sl = i % 3
            T.dma("pool", out=wbf[sl][:], in_=w_in_d[:, 128 * c:128 * (c + 1)].rearrange("(p k) c -> p k c", k=8),
                  writes=[("wbf", sl)], semkey=f"wbf{sl}")

        def next_weights():
            i = wstate["loaded"]
            if i == 0:
                load_chunk(0)
                load_chunk(1)
            load_chunk(i + 2)
            wstate["loaded"] = i + 1
            return i % 3

        cnt = {"tmp": 0, "evac": 0, "pb": 0}

        def gen_proj(kind, dst, dst_key):
            sl = next_weights()
            for n in range(4):
                pbl = cnt.get("pbanks", (0, 1))
                b = pbl[cnt["pb"] % len(pbl)]
                cnt["pb"] += 1
                for k in range(8):
                    T.op("pe", lambda e, b=b, k=k, n=n, sl=sl: e.matmul(bank(b), lhsT=wbf[sl][:, k, :], rhs=hT[:, k, 512 * n:512 * (n + 1)], start=(k == 0), stop=(k == 7)),
                         reads=[("wbf", sl), "hT"], writes=[("ps", b)], sig=(k == 7))
                    if k % int(_K("STEPK", 2)) == int(_K("STEPK", 2)) - 1 and k < 7:
                        yield
                d = dst[:, 512 * n:512 * (n + 1)]
                src = bank(b)
                if kind == "cast":
                    cnt["evac"] += 1
                    if cnt["evac"] % 2 == 0:
                        T.op("act", lambda e, d=d, src=src: e.activation(out=d, in_=src, func=AF.Copy), reads=[("ps", b)], writes=[dst_key])
                    else:
                        T.op("dve", lambda e, d=d, src=src: e.tensor_copy(out=d, in_=src), reads=[("ps", b)], writes=[dst_key])
                elif kind in ("silu2", "silu2_mul"):
                    ti = cnt["tmp"] % 4
                    cnt["tmp"] += 1
                    t1 = tmp[ti]
                    T.op("act", lambda e, t1=t1, src=src: e.activation(out=t1[:], in_=src, func=AF.Tanh, scale=0.5), reads=[("ps", b)], writes=[("tmp", ti)])
                    if kind == "silu2":
                        T.op("dve", lambda e, t1=t1, src=src, d=d: e.scalar_tensor_tensor(out=d, in0=t1[:], scalar=1.0, in1=src, op0=ALU.add, op1=ALU.mult),
                             reads=[("ps", b), ("tmp", ti)], writes=[dst_key])
                    else:
                        T.op("dve", lambda e, t1=t1, src=src: e.scalar_tensor_tensor(out=t1[:], in0=t1[:], scalar=1.0, in1=src, op0=ALU.add, op1=ALU.mult),
                             reads=[("ps", b), ("tmp", ti)], writes=[("tmp", ti)])
                        T.op("pool", lambda e, t1=t1, d=d: e.tensor_tensor(out=d, in0=d, in1=t1[:], op=ALU.mult),
                             reads=[("tmp", ti), dst_key], writes=[dst_key])
                elif kind == "gelu2":
                    ti = cnt["tmp"] % 4
                    tj = (cnt["tmp"] + 1) % 4
                    cnt["tmp"] += 2
                    t1, t2 = tmp[ti], tmp[tj]
                    T.op("act", lambda e, t1=t1, src=src: e.activation(out=t1[:], in_=src, func=AF.Square, scale=0.044715 ** 0.5), reads=[("ps", b)], writes=[("tmp", ti)])
                    T.op("dve", lambda e, t1=t1, t2=t2, src=src: e.scalar_tensor_tensor(out=t2[:], in0=t1[:], scalar=1.0, in1=src, op0=ALU.add, op1=ALU.mult),
                         reads=[("ps", b), ("tmp", ti)], writes=[("tmp", tj)])
                    T.op("act", lambda e, t2=t2: e.activation(out=t2[:], in_=t2[:], func=AF.Tanh, scale=GELU_C), reads=[("tmp", tj)], writes=[("tmp", tj)])
                    T.op("dve", lambda e, t2=t2, src=src, d=d: e.scalar_tensor_tensor(out=d, in0=t2[:], scalar=1.0, in1=src, op0=ALU.add, op1=ALU.mult),
                         reads=[("ps", b), ("tmp", tj)], writes=[dst_key])
                yield

        def interleave(main, side, q):
            acc = 0.0
            for _ in main:
                acc += q
                while side is not None and acc >= 1.0:
                    acc -= 1.0
                    next(side, None)
            if side is not None:
                for _ in side:
                    pass

        def norm_transpose(src_ap_fn, ntiles, use_g, dstT, dst_key, sskey, ss, lnv, rstd):
            dkeys = dst_key if isinstance(dst_key, list) else [dst_key]
            PRE = 3

            def slot(t):
                q = t % 4
                return q, q // 2, xin[q // 2][:, 1024 * (q % 2):1024 * (q % 2) + D]

            def stA(t):
                q, sl, xv = slot(t)
                T.dma("sp", out=xv, in_=src_ap_fn(t), writes=([("xin", sl), ("xq", q)] if t < 4 else [("xq", q)]), semkey=f"xq{q}")
                T.op("act", lambda e, xv=xv, t=t: e.activation(out=junk[:, :], in_=xv, func=AF.Square, accum_out=ss[:, t:t + 1]),
                     reads=[("xq", q)], writes=[("tmp", 0), (sskey, t)])
                T.op("act", lambda e, t=t: e.activation(out=lnv[:, t:t + 1], in_=ss[:, t:t + 1], func=AF.Ln, bias=EPS, scale=1.0 / D),
                     reads=[(sskey, t)], writes=[(sskey + "l", t)])
                T.op("act", lambda e, t=t: e.activation(out=rstd[:, t:t + 1], in_=lnv[:, t:t + 1], func=AF.Exp, scale=-0.5),
                     reads=[(sskey + "l", t)], writes=[(sskey + "r", t)])

            def stB(t):
                q, sl, xv = slot(t)
                hs = t % 2
                hb = work[hs][:, 0:D]
                rk = [("xq", q), (sskey + "r", t)] + ([("xin", sl)] if t >= ntiles - 4 else [])
                if use_g:
                    T.op("dve", lambda e, hb=hb, xv=xv, t=t: e.scalar_tensor_tensor(out=hb, in0=xv, scalar=rstd[:, t:t + 1], in1=gn_bc[:], op0=ALU.mult, op1=ALU.mult),
                         reads=rk + ["gn_bc"], writes=[("work", hs)])
                else:
                    T.op("dve", lambda e, hb=hb, xv=xv, t=t: e.tensor_scalar(out=hb, in0=xv, scalar1=rstd[:, t:t + 1], scalar2=None, op0=ALU.mult),
                         reads=rk, writes=[("work", hs)])
                tb = 2 + t % 6
                for k in range(8):
                    T.op("pe", lambda e, tb=tb, k=k, hb=hb: e.transpose(out=bankb(tb)[:, 128 * k:128 * (k + 1)], in_=hb[:, k:D:8], identity=identb[:]),
                         reads=[("work", hs), "identb"], writes=[("ps", tb)], sig=(k == 7))

            def stC(t):
                tb = 2 + t % 6
                if t % 8 == 0:
                    T.op("act", lambda e, tb=tb, t=t: e.activation(out=dstT[:, :, 128 * t:128 * (t + 1)], in_=bankb(tb).rearrange("p (k t) -> p k t", k=8), func=AF.Copy),
                         reads=[("ps", tb)], writes=dkeys)
                else:
                    T.op("dve", lambda e, tb=tb, t=t: e.tensor_copy(out=dstT[:, :, 128 * t:128 * (t + 1)], in_=bankb(tb).rearrange("p (k t) -> p k t", k=8)),
                         reads=[("ps", tb)], writes=dkeys)

            for t in range(min(PRE, ntiles)):
                stA(t)
            for t in range(ntiles):
                if t + PRE < ntiles:
                    stA(t + PRE)
                stB(t)
                if t > 0:
                    stC(t - 1)
            stC(ntiles - 1)

        rkeys = [("rL", i) for i in range(4)]
        memT = rL.bitcast(BF16)[:, 0:8 * NMEM].rearrange("p (k m) -> p k m", k=8)

        def P_B(W):
            for j in range(2):
                yield from gen_proj("gelu2", W[j][:, :], ("work", W[j].wkey))
            for j in range(2):
                yield from gen_proj("gelu2", W[2 + j][:, :], ("work", W[2 + j].wkey))
            for j in range(2):
                yield from gen_proj("silu2_mul", W[2 + j][:, :], ("work", W[2 + j].wkey))

        def Mx_B(W):
            for q4 in range(4):
                tb = 2 + q4 % 3
                for t4 in range(4):
                    tt = 4 * q4 + t4
                    for j in range(2):
                        T.op("pe", lambda e, tb=tb, t4=t4, j=j, tt=tt: e.transpose(out=bankb(tb)[:, 256 * t4 + 128 * j:256 * t4 + 128 * (j + 1)], in_=W[j][:, 128 * tt:128 * (tt + 1)], identity=identb[:]),
                             reads=[("work", W[j].wkey), "identb"], writes=[("ps", tb)], sig=(t4 == 3 and j == 1))
                T.op("dve", lambda e, tb=tb, q4=q4: e.tensor_copy(out=vraw_v[:, 4 * q4:4 * q4 + 4, :], in_=bankb(tb).rearrange("p (t c) -> p t c", c=256)),
                     reads=[("ps", tb)], writes=[("xin", 0)])
                for t4 in range(4):
                    tt = 4 * q4 + t4
                    T.op("act", lambda e, tt=tt: e.activation(out=junk[:, 0:256], in_=vraw_v[:, tt, :], func=AF.Square, accum_out=ss2[:, tt:tt + 1]),
                         reads=[("xin", 0)], writes=[("tmp", 0), "ss2"])
                yield
            T.op("act", lambda e: e.activation(out=lnv2[:], in_=ss2[:], func=AF.Ln, bias=4.0 * EPS, scale=1.0 / 256.0), reads=["ss2"], writes=["lnv2"])
            T.op("act", lambda e: e.activation(out=rstd2[:], in_=lnv2[:], func=AF.Exp, scale=-0.5), reads=["lnv2"], writes=["rstd2"])
            for tt in range(16):
                T.op("dve", lambda e, tt=tt: e.tensor_scalar(out=vraw_v[:, tt, :], in0=vraw_v[:, tt, :], scalar1=rstd2[:, tt:tt + 1], scalar2=None, op0=ALU.mult),
                     reads=[("xin", 0), "rstd2"], writes=[("xin", 0)])
                if tt % 4 == 3:
                    yield
            for n in range(4):
                for j in range(2):
                    pb = 5 + j
                    for q in range(4):
                        tt = 4 * n + q
                        for gg in range(2):
                            g = 2 * j + gg
                            T.op("pe", lambda e, pb=pb, q=q, gg=gg, g=g, tt=tt: e.matmul(bank(pb)[64 * gg:64 * gg + 64, 128 * q:128 * (q + 1)], lhsT=vraw_v[:, tt, 64 * g:64 * (g + 1)], rhs=WsT[:, g, :], start=True, stop=True),
                                 reads=[("xin", 0), "WsT"], writes=[("ps", pb)], sig=(q == 3 and gg == 1))
                    ti = cnt["tmp"] % 4
                    cnt["tmp"] += 1
                    t1 = tmp[ti]
                    bs_bc = bass.AP(bsT, j * 128, [[256, 128], [0, 4], [1, 128]])
                    T.op("dve", lambda e, pb=pb, t1=t1, j=j, bs_bc=bs_bc: e.scalar_tensor_tensor(out=t1[:].rearrange("p (q t) -> p q t", q=4), in0=bank(pb).rearrange("p (q t) -> p q t", q=4), scalar=gvT[:, j:j + 1], in1=bs_bc, op0=ALU.mult, op1=ALU.add),
                         reads=[("ps", pb), "gvT", "bsT"], writes=[("tmp", ti)])
                    T.op("pool", lambda e, t1=t1, j=j, n=n: e.tensor_tensor(out=gatedT[:, 4 + j, 512 * n:512 * (n + 1)], in0=t1[:], in1=W[2 + j][:, 512 * n:512 * (n + 1)], op=ALU.mult),
                         reads=[("tmp", ti), ("work", W[2 + j].wkey)], writes=[("gatedT", 4 + j)])
                    yield

        def P_M(W):
            for j in range(2):
                yield from gen_proj("cast", W[j][:, :], ("work", W[j].wkey))
            for j in range(2):
                yield from gen_proj("silu2", W[2 + j][:, :], ("work", W[2 + j].wkey))

        def Mx_M(W):
            munits = [(j, n, i, hh) for j in range(2) for n in range(4) for i in range(2) for hh in range(2)]
            MB = [2, 3, 4] if _K("MB4", 0) == 0 else [2, 3, 4, 7]
            MLOOK = len(MB) - 1

            def m_qk(ui):
                j, n, i, hh = munits[ui]
                sbk = MB[ui % len(MB)]
                T.op("pe", lambda e, sbk=sbk, hh=hh, j=j, i=i, n=n: e.matmul(bank(sbk), lhsT=kmT[64 * hh:64 * hh + 64, j, 128 * i:128 * (i + 1)], rhs=W[j][64 * hh:64 * hh + 64, 512 * n:512 * (n + 1)], start=True, stop=True),
                     reads=["kmT", ("work", W[j].wkey)], writes=[("ps", sbk)])

            def m_exp(ui):
                j, n, i, hh = munits[ui]
                sbk = MB[ui % len(MB)]
                es_ = ui % 4
                T.op("act", lambda e, sbk=sbk, es_=es_: e.activation(out=ebuf[es_][:], in_=bank(sbk), func=AF.Exp, scale=0.125),
                     reads=[("ps", sbk)], writes=[("ebuf", es_)])

            def m_pv(ui):
                j, n, i, hh = munits[ui]
                es_ = ui % 4
                T.op("pe", lambda e, hh=hh, i=i, j=j, es_=es_: e.matmul(bank(5 + hh), lhsT=vmtok[:, i, j, 64 * hh:64 * hh + 128], rhs=ebuf[es_][:], start=(i == 0), stop=(i == 1)),
                     reads=["vmtok", ("ebuf", es_)], writes=[("ps", 5 + hh)])

            def m_norm(j, n, mc):
                rs = (mc // 2) % 4
                ti = cnt["tmp"] % 4
                cnt["tmp"] += 1
                cs = slice(512 * rs, 512 * (rs + 1))
                for hh in range(2):
                    po = slice(64 * hh, 64 * hh + 64)
                    pl = slice(64 - 64 * hh, 128 - 64 * hh)
                    T.op("dve", lambda e, hh=hh, po=po, j=j, n=n: e.tensor_tensor(out=tmp[ti][po, :], in0=bank(5 + hh)[po, :], in1=W[2 + j][po, 512 * n:512 * (n + 1)], op=ALU.mult),
                         reads=[("ps", 5 + hh), ("work", W[2 + j].wkey)], writes=[("tmp", ti)])
                    T.op("act", lambda e, hh=hh, po=po, pl=pl: e.activation(out=rL[po, cs], in_=bank(5 + hh)[pl, :], func=AF.Ln),
                         reads=[("ps", 5 + hh)], writes=[("rL", rs)])
                T.op("act", lambda e: e.activation(out=rL[:, cs], in_=rL[:, cs], func=AF.Exp, scale=-1.0),
                     reads=[("rL", rs)], writes=[("rL", rs)])
                T.op("pool", lambda e, j=j, n=n: e.tensor_tensor(out=gatedT[:, 6 + j, 512 * n:512 * (n + 1)], in0=tmp[ti][:, :], in1=rL[:, cs], op=ALU.mult),
                     reads=[("tmp", ti), ("rL", rs)], writes=[("gatedT", 6 + j)])

            NM = len(munits)
            mcount = 0
            for ui in range(MLOOK):
                m_qk(ui)
            for ui in range(NM):
                if ui + MLOOK < NM:
                    m_qk(ui + MLOOK)
                m_exp(ui)
                yield
                m_pv(ui)
                j, n, i, hh = munits[ui]
                if i == 1 and hh == 1:
                    m_norm(j, n, mcount)
                    mcount += 2

        def P_A(W):
            yield from gen_proj("cast", W[0][:, :], ("work", W[0].wkey))
            yield from gen_proj("cast", W[1][:, :], ("work", W[1].wkey))
            yield from gen_proj("cast", W[2][:, :], ("work", W[2].wkey))
            yield from gen_proj("silu2", W[3][:, :], ("work", W[3].wkey))

        def Mx_A(W, hp):
            qT, kT, vT, zT = W[0], W[1], W[2], W[3]
            kq, kk, kv, kz = [("work", W[i].wkey) for i in range(4)]
            for c in range(3):
                d = DILS[c]
                npr = 16 // d
                for q4 in range(4):
                    vb = (2, 3, 4, 7)[q4]
                    for t4 in range(4):
                        tau = 4 * q4 + t4
                        r, jj = tau // npr, tau % npr
                        t0 = d * (128 * jj) + r
                        T.op("pe", lambda e, t4=t4, t0=t0, d=d, vb=vb: e.transpose(out=bankb(vb)[:, 128 * t4:128 * (t4 + 1)], in_=vT[:, t0:t0 + d * 127 + 1:d], identity=identb[:]),
                             reads=[kv, "identb"], writes=[("ps", vb)], sig=(t4 == 3))
                    o_ap = vtok[:, c * 16 + 4 * q4:c * 16 + 4 * q4 + 4, :].rearrange("p t (c e) -> p t c e", c=3)[:, :, 0:3:2, :]
                    i_ap = bankb(vb)[:, 0:512].rearrange("p (t c e) -> p t c e", t=4, c=2)
                    if q4 % 2 == 0:
                        T.op("dve", lambda e, o_ap=o_ap, i_ap=i_ap: e.tensor_copy(out=o_ap, in_=i_ap), reads=[("ps", vb)], writes=["vtok"])
                    else:
                        T.op("act", lambda e, o_ap=o_ap, i_ap=i_ap: e.activation(out=o_ap, in_=i_ap, func=AF.Copy), reads=[("ps", vb)], writes=["vtok"])
                    yield
            units = []
            for c in range(2):
                d = DILS[c]
                npr = 16 // d
                for blk in range(4):
                    ulist = []
                    if c == 0:
                        segs = [(0, d, 0, 512 * blk, 512 * blk + 512, 0, 1)]
                    else:
                        segs = [(1, 4, blk, 0, 512, 0, 1)] + [(2, 16, blk + 4 * j, 0, 128, j, 4) for j in range(4)]
                    for (cc, dd, r, l0, l1, aoff, cstep) in segs:
                        nprr = 16 // dd
                        for jj in range(nprr):
                            qlo = max(l0, 128 * jj - 64)
                            qhi = min(l1, 128 * jj + 192)
                            if qhi <= qlo:
                                continue
                            ulist.append(dict(c=cc, d=dd, r=r, jj=jj, qlo=qlo, n=qhi - qlo, acol=aoff + cstep * (qlo - l0), cstep=cstep,
                                              moff=qlo - (128 * jj - 64), tau=cc * 16 + r * nprr + jj, blk=blk, ec=c))
                    for ui, u in enumerate(ulist):
                        u["first"] = (ui == 0)
                        u["last"] = (ui == len(ulist) - 1)
                    units += ulist

            ST_BANKS = [(2, 3), (4, 7)]
            NST = len(ST_BANKS)
            NEB = 4

            def emit_qk_h(ui, hh):
                u = units[ui]
                sbk = ST_BANKS[ui % NST][hh]
                d, r, n = u["d"], u["r"], u["n"]
                k0 = d * (128 * u["jj"]) + r
                q0 = d * u["qlo"] + r
                T.op("pe", lambda e, sbk=sbk, hh=hh, k0=k0, q0=q0, d=d, n=n: e.matmul(bank(sbk)[:, 0:n], lhsT=kT[64 * hh:64 * hh + 64, k0:k0 + d * 127 + 1:d], rhs=qT[64 * hh:64 * hh + 64, q0:q0 + d * (n - 1) + 1:d], start=True, stop=True),
                     reads=[kq, kk], writes=[("ps", sbk)])

            def emit_softmax(ui):
                u = units[ui]
                bks = ST_BANKS[ui % NST]
                es_ = ui % NEB
                n = u["n"]
                ev = ebuf[es_][:].rearrange("p (h q) -> p h q", h=2)[:, :, 0:n]
                sv = bass.AP(PS, 512 * bks[0], [[4096, 128], [512 * (bks[1] - bks[0]), 2], [1, n]])
                mv = masks[:, u["c"] * 4 + hp, :, u["moff"]:u["moff"] + n]
                T.op("act", lambda e, ev=ev, sv=sv: e.activation(out=ev, in_=sv, func=AF.Exp, scale=0.125),
                     reads=[("ps", bks[0]), ("ps", bks[1])], writes=[("ebuf", es_)])
                T.op("dve", lambda e, ev=ev, mv=mv: e.tensor_tensor(out=ev, in0=ev, in1=mv, op=ALU.mult),
                     reads=[("ebuf", es_), "masks"], writes=[("ebuf", es_)])

            def emit_pv_h(ui, hh):
                u = units[ui]
                es_ = ui % NEB
                n = u["n"]
                T.op("pe", lambda e, hh=hh, u=u, es_=es_, n=n: e.matmul(bank(5 + hh)[:, u["acol"]:u["acol"] + u["cstep"] * (n - 1) + 1:u["cstep"]], lhsT=vtok[:, u["tau"], 64 * hh:64 * hh + 128], rhs=ebuf[es_][:, 256 * hh:256 * hh + n], start=u["first"], stop=u["last"], skip_group_check=True),
                     reads=["vtok", ("ebuf", es_)], writes=[("ps", 5 + hh)])

            def emit_evac(ui):
                u = units[ui]
                if u["last"]:
                    c, blk = u["ec"], u["blk"]
                    for hh in range(2):
                        oa = xin[hh]
                        if c == 0:
                            dst = oa[:, 512 * blk:512 * (blk + 1)]
                            src = bank(5 + hh)
                            if hh == 0:
                                T.op("act", lambda e, dst=dst, src=src: e.activation(out=dst, in_=src, func=AF.Copy), reads=[("ps", 5 + hh)], writes=[("xin", hh)])
                            else:
                                T.op("dve", lambda e, dst=dst, src=src: e.tensor_copy(out=dst, in_=src), reads=[("ps", 5 + hh)], writes=[("xin", hh)])
                        else:
                            if c == 1:
                                dst = oa[:].rearrange("p (l r) -> p l r", r=4)[:, :, blk]
                                src = bank(5 + hh)
                            else:
                                dst = oa[:].rearrange("p (l r) -> p r l", r=16)[:, 4 * blk:4 * blk + 4, :]
                                src = bank(5 + hh).rearrange("p (r l) -> p r l", r=4)
                            if hh == 0:
                                T.op("dve", lambda e, dst=dst, src=src: e.tensor_tensor(out=dst, in0=src, in1=dst, op=ALU.add),
                                     reads=[("ps", 5 + hh), ("xin", hh)], writes=[("xin", hh)])
                            else:
                                ti = cnt["tmp"] % 4
                                cnt["tmp"] += 1
                                tv = tmp[ti][:] if c == 1 else tmp[ti][:].rearrange("p (r l) -> p r l", r=4)
                                T.op("act", lambda e, tv=tv, src=src: e.activation(out=tv, in_=src, func=AF.Copy),
                                     reads=[("ps", 5 + hh)], writes=[("tmp", ti)])
                                deferred.append(lambda dst=dst, tv=tv, ti=ti, hh=hh: T.op("dve", lambda e: e.tensor_tensor(out=dst, in0=tv, in1=dst, op=ALU.add),
                                                reads=[("tmp", ti), ("xin", hh)], writes=[("xin", hh)]))

            NU = len(units)
            LOOK = NST - 1
            deferred = []
            for hh in range(2):
                for ui in range(min(LOOK, NU)):
                    emit_qk_h(ui, hh)
            for ui in range(NU):
                if ui + LOOK < NU:
                    emit_qk_h(ui + LOOK, 0)
                    emit_qk_h(ui + LOOK, 1)
                emit_softmax(ui)
                while deferred:
                    deferred.pop(0)()
                yield
                emit_pv_h(ui, 0)
                emit_pv_h(ui, 1)
                emit_evac(ui)
            while deferred:
                deferred.pop(0)()
            for hh in range(2):
                po = slice(64 * hh, 64 * hh + 64)
                pl = slice(64 - 64 * hh, 128 - 64 * hh)
                T.op("act", lambda e, hh=hh, po=po, pl=pl: e.activation(out=rL[po, :], in_=xin[hh][pl, :], func=AF.Ln),
                     reads=[("xin", hh)], writes=rkeys)
            T.op("act", lambda e: e.activation(out=rL[:, :], in_=rL[:, :], func=AF.Exp, scale=-1.0),
                 reads=rkeys, writes=rkeys)
            yield
            T.op("pool", lambda e: e.tensor_tensor(out=rL[:, :], in0=rL[:, :], in1=zT[:, :], op=ALU.mult),
                 reads=rkeys + [kz], writes=rkeys)
            for hh in range(2):
                po = slice(64 * hh, 64 * hh + 64)
                T.op("pool", lambda e, hh=hh, po=po: e.tensor_tensor(out=gatedT[po, hp, :], in0=xin[hh][po, :], in1=rL[po, :], op=ALU.mult),
                     reads=[("xin", hh)] + rkeys, writes=[("gatedT", hp)])
            yield

        class WS:
            def __init__(self, t, k):
                self.t = t
                self.wkey = k

            def __getitem__(self, idx):
                return self.t[idx]

        WSETS = [[WS(work[4 * s_ + i], 4 * s_ + i) for i in range(4)] for s_ in range(2)]

        setup_early()
        for b in range(NSEQ):
            xb = b * SEQ
            norm_transpose(lambda t: x_d[xb + 128 * t:xb + 128 * (t + 1), :],
                           16, True, hT, "hT", "ssx", ss, lnv, rstd)
            if b == 0:
                setup_mid()
            norm_transpose(lambda t: mem_d[b * NMEM + 128 * t:b * NMEM + 128 * (t + 1), :],
                           2, False, memT, rkeys, "ssm", ssm, lnvm, rstdm)
            for j in range(2):
                for k in range(8):
                    T.op("pe", lambda e, j=j, k=k: e.matmul(bank(j)[:, 0:NMEM], lhsT=wkv[:, k, 128 * j:128 * (j + 1)], rhs=memT[:, k, :], start=(k == 0), stop=(k == 7)),
                         reads=["wkv"] + rkeys, writes=[("ps", j)], sig=(k == 7))
                T.op("dve", lambda e, j=j: e.tensor_copy(out=kmT[:, j, :], in_=bank(j)[:, 0:NMEM]), reads=[("ps", j)], writes=["kmT"])
            for i in range(2):
                for k in range(8):
                    T.op("pe", lambda e, i=i, k=k: e.matmul(bank(i)[:, 0:256], lhsT=memT[:, k, 128 * i:128 * (i + 1)], rhs=wkv[:, k, 256:512], start=(k == 0), stop=(k == 7)),
                         reads=["wkv"] + rkeys, writes=[("ps", i)], sig=(k == 7))
                o_ap = vmtok[:, i, :, :].rearrange("p j (c e) -> p j c e", c=3)[:, :, 0:3:2, :]
                i_ap = bank(i)[:, 0:256].rearrange("p (j c e) -> p j c e", j=2, c=2)
                T.op("dve", lambda e, o_ap=o_ap, i_ap=i_ap: e.tensor_copy(out=o_ap, in_=i_ap), reads=[("ps", i)], writes=["vmtok"])

            jobs = [("A", hp) for hp in range(4)] + [("M", None), ("B", None)]

            def mkP(ji):
                kind, hp = jobs[ji]
                W = WSETS[ji % 2]
                return {"B": P_B, "M": P_M, "A": P_A}[kind](W)

            def mkMx(ji):
                kind, hp = jobs[ji]
                W = WSETS[ji % 2]
                if kind == "B":
                    return Mx_B(W), 4.0
                if kind == "M":
                    return Mx_M(W), 2.0
                return Mx_A(W, hp), _K("RA", 1.0)

            interleave(mkP(0), setup_late() if b == 0 else None, 0.55)
            for ji in range(len(jobs)):
                main, R = mkMx(ji)
                side = mkP(ji + 1) if ji + 1 < len(jobs) else None
                if side is not None and jobs[ji + 1][0] == "B":
                    R = _K("RM", 3.0)
                    cnt["pbanks"] = (0, 1, 7) if _K("MB4", 0) == 0 else (0, 1)
                interleave(main, side, R)
                cnt["pbanks"] = (0, 1)

            gkeys = [("gatedT", k) for k in range(8)]
            for tt in range(16):
                q = tt % 4
                sl = q // 2
                xr = xin[sl][:, 1024 * (q % 2):1024 * (q % 2) + D]
                kx = ("xr", q)
                T.dma("sp", out=xr, in_=x_d[xb + 128 * tt:xb + 128 * (tt + 1), :], writes=([("xin", sl), kx] if tt < 4 else [kx]), semkey=f"xr{q}")
                ob = 2 * (tt % 3)
                for half in range(2):
                    for k in range(8):
                        T.op("pe", lambda e, half=half, k=k, tt=tt, ob=ob: e.matmul(bank(ob + half), lhsT=gatedT[:, k, 128 * tt:128 * (tt + 1)], rhs=wout[:, k, 512 * half:512 * (half + 1)], start=(k == 0), stop=(k == 7)),
                             reads=gkeys + ["wout"], writes=[("ps", ob + half)], sig=(k == 7))
                T.op("dve", lambda e, xr=xr, ob=ob: e.tensor_tensor(out=xr, in0=PS[:, 512 * ob:512 * ob + 1024], in1=xr, op=ALU.add),
                     reads=[("ps", ob), ("ps", ob + 1), kx], writes=[kx])
                T.op("act", lambda e, xr=xr, tt=tt: e.activation(out=junk[:, :], in_=xr, func=AF.Square, accum_out=sso[:, tt:tt + 1]),
                     reads=[kx], writes=[("tmp", 0), ("sso", tt)])
                T.op("act", lambda e, tt=tt: e.activation(out=lnvo[:, tt:tt + 1], in_=sso[:, tt:tt + 1], func=AF.Ln, bias=EPS, scale=1.0 / D),
                     reads=[("sso", tt)], writes=[("ssol", tt)])
                T.op("act", lambda e, tt=tt: e.activation(out=rstdo[:, tt:tt + 1], in_=lnvo[:, tt:tt + 1], func=AF.Exp, scale=-0.5),
                     reads=[("ssol", tt)], writes=[("ssor", tt)])
                T.op("dve", lambda e, xr=xr, tt=tt: e.scalar_tensor_tensor(out=xr, in0=xr, scalar=rstdo[:, tt:tt + 1], in1=gfin_bc[:], op0=ALU.mult, op1=ALU.mult),
                     reads=[kx, ("ssor", tt), "gfin_bc"], writes=[kx])
                T.dma("pool", out=out_d[xb + 128 * tt:xb + 128 * (tt + 1), :], in_=xr, reads=([kx, ("xin", sl)] if tt >= 12 else [kx]), semkey=f"out{q}")

        T.finish("pool", ["out0", "out1", "out2", "out3"])
        print("instructions emitted:", T.nins, {k: v[1] for k, v in T.sems.items()})
    return nc


_CACHE = {}


def _consts():
    k = np.arange(128)[:, None]
    q = np.arange(256)[None, :]
    rel = np.abs(q - 64 - k).astype(np.float32)
    band = (rel <= 64).astype(np.float32)
    return np.eye(128, dtype=np.float32), rel, band


def kernel(x, mem, g_norm, w_in, w_sgu_spatial, b_sgu_spatial, g_sgu_v, g_mem, w_mem_kv, w_out, g_final):
    f = lambda a: np.ascontiguousarray(np.asarray(a, dtype=np.float32))
    x = f(x); mem = f(mem)
    if "nc" not in _CACHE:
        _CACHE["nc"] = build_program()
    nc = _CACHE["nc"]
    ident, rel, band = _consts()
    shared = {
        "g_norm": f(g_norm).reshape(D), "w_in": f(w_in).reshape(D, INC), "w_s": f(w_sgu_spatial).reshape(4, 128, 128),
        "b_s": f(b_sgu_spatial).reshape(4, 128), "g_v": f(g_sgu_v).reshape(256), "g_mem": f(g_mem).reshape(D),
        "w_kv": f(w_mem_kv).reshape(D, 512), "w_out": f(w_out).reshape(D, D), "g_final": f(g_final).reshape(D),
        "ident": ident, "relabs": rel, "band": band,
    }
    in_maps = []
    for c in range(NCORES):
        m = dict(shared)
        m["x"] = x[NSEQ * c:NSEQ * (c + 1)].reshape(NSEQ * SEQ, D)
        m["mem"] = mem[NSEQ * c:NSEQ * (c + 1)].reshape(NSEQ * NMEM, D)
        in_maps.append(m)
    res = run_bass_kernel_spmd(nc, in_maps, core_ids=list(range(NCORES)))
    out = np.concatenate([r["out"].reshape(NSEQ, SEQ, D) for r in res.results], axis=0)
    return out.astype(np.float32)
```
